# Optimizing a Trainium2 kernel written in Bass

```python
import math
import jax, jax.numpy as jnp
from jax import lax
import numpy as np

D_MODEL = 1024
BATCH = 16
SEQ = 4096
DEPTH = 2
DEC_BATCH = 16
DEC_SEQ = 16
PAST_LEN = 4096

CHUNK = 64
SSD_D_INNER = 2 * D_MODEL
SSD_HEAD_DIM = 64
SSD_N_HEADS = SSD_D_INNER // SSD_HEAD_DIM
SSD_N_GROUPS = 8
SSD_D_STATE = 128
SSD_CONV = 4
SSD_CONV_DIM = SSD_D_INNER + 2 * SSD_N_GROUPS * SSD_D_STATE
ATT_HEAD_DIM = 64
ATT_HEADS = D_MODEL // ATT_HEAD_DIM
ATT_KV_HEADS = ATT_HEADS // 4
ATT_GROUP = ATT_HEADS // ATT_KV_HEADS
WINDOW = 128
N_BUCKETS = 32
MAX_DISTANCE = 128
SC_DIM = D_MODEL
SC_WIDTH = 3
N_EXPERTS = 32
TOP_K = 4
D_FF = D_MODEL
SWIGLU_LIMIT = 7.0
SWIGLU_ALPHA = 1.702
MOE_BLOCK = 128
LN_EPS = 1e-5
ALPHA = (2.0 * DEPTH) ** 0.25
BETA = (8.0 * DEPTH) ** -0.25
IN_SIZES = (SSD_D_INNER, SSD_CONV_DIM, SSD_N_HEADS,
            ATT_HEADS * ATT_HEAD_DIM, ATT_KV_HEADS * ATT_HEAD_DIM, ATT_KV_HEADS * ATT_HEAD_DIM,
            SC_DIM, SC_DIM, SC_DIM,
            D_MODEL, D_MODEL, D_MODEL)
D_IN = sum(IN_SIZES)

kernel_name = 'hybrid_ssd_swa_shortconv_moe_stream_step'


def _split_points():
    return [int(v) for v in np.cumsum(IN_SIZES)[:-1]]


def layer_norm(x, g, b):
    xf = x.astype(jnp.float32)
    mu = jnp.mean(xf, -1, keepdims=True)
    var = jnp.mean(jnp.square(xf - mu), -1, keepdims=True)
    return ((xf - mu) * lax.rsqrt(var + LN_EPS) * g.astype(jnp.float32) + b.astype(jnp.float32)).astype(x.dtype)


def group_rmsnorm(y, w):
    b, s, d = y.shape
    yf = y.astype(jnp.float32).reshape(b, s, SSD_N_GROUPS, d // SSD_N_GROUPS)
    yf = yf * lax.rsqrt(jnp.mean(yf * yf, -1, keepdims=True) + LN_EPS)
    return (yf.reshape(b, s, d) * w.astype(jnp.float32)).astype(y.dtype)


def causal_conv(u, hist, w):
    width = w.shape[0]
    s = u.shape[1]
    up = jnp.concatenate([hist, u], axis=1)
    y = up[:, 0:s] * w[0]
    for j in range(1, width):
        y = y + up[:, j:j + s] * w[j]
    return y, up[:, s:]


def ssd_scan(xdt, adt, bm, cm, init, chunk):
    b, s, h, p = xdt.shape
    g, n = SSD_N_GROUPS, SSD_D_STATE
    r = h // g
    nc = s // chunk
    f32 = jnp.float32
    x = xdt.astype(f32).reshape(b, nc, chunk, g, r, p)
    a = adt.astype(f32).reshape(b, nc, chunk, g, r)
    bc = bm.astype(f32).reshape(b, nc, chunk, g, n)
    cc = cm.astype(f32).reshape(b, nc, chunk, g, n)
    a_cs = jnp.cumsum(a, axis=2)
    seg = a_cs[:, :, :, None] - a_cs[:, :, None, :]
    causal = jnp.tril(jnp.ones((chunk, chunk), dtype=bool))[:, :, None, None]
    decay = jnp.exp(jnp.where(causal, seg, -jnp.inf))
    cb = jnp.einsum('bclgn,bcsgn->bclsg', cc, bc)
    y_diag = jnp.einsum('bclsg,bclsgr,bcsgrp->bclgrp', cb, decay, x)
    decay_to_end = jnp.exp(a_cs[:, :, -1:] - a_cs)
    chunk_states = jnp.einsum('bclgn,bclgr,bclgrp->bcgrpn', bc, decay_to_end, x)
    chunk_decay = jnp.exp(a_cs[:, :, -1])

    def step(carry, inp):
        st, dec = inp
        return carry * dec[..., None, None] + st, carry

    init_f = init.astype(f32).reshape(b, g, r, p, n)
    final, prev = lax.scan(step, init_f, (jnp.moveaxis(chunk_states, 1, 0), jnp.moveaxis(chunk_decay, 1, 0)))
    prev = jnp.moveaxis(prev, 0, 1)
    y_off = jnp.einsum('bclgn,bcgrpn,bclgr->bclgrp', cc, prev, jnp.exp(a_cs))
    y = (y_diag + y_off).reshape(b, s, h, p)
    return y, final.reshape(b, h, p, n)


def ssd_branch(z, xbc, dt_raw, hist, init_state, conv_w, conv_b, dt_bias, a_log, d_skip, norm_w, w_out, chunk):
    b, s, _ = z.shape
    xbc_c, new_hist = causal_conv(xbc, hist, conv_w)
    xbc_c = jax.nn.silu(xbc_c + conv_b)
    gn = SSD_N_GROUPS * SSD_D_STATE
    xs = xbc_c[..., :SSD_D_INNER].reshape(b, s, SSD_N_HEADS, SSD_HEAD_DIM)
    bm = xbc_c[..., SSD_D_INNER:SSD_D_INNER + gn].reshape(b, s, SSD_N_GROUPS, SSD_D_STATE)
    cm = xbc_c[..., SSD_D_INNER + gn:].reshape(b, s, SSD_N_GROUPS, SSD_D_STATE)
    dt = jax.nn.softplus(dt_raw.astype(jnp.float32) + dt_bias.astype(jnp.float32))
    a = -jnp.exp(a_log.astype(jnp.float32))
    y, final = ssd_scan(xs.astype(jnp.float32) * dt[..., None], dt * a, bm, cm, init_state, chunk)
    y = y + d_skip.astype(jnp.float32)[:, None] * xs.astype(jnp.float32)
    y = y.astype(z.dtype).reshape(b, s, SSD_D_INNER) * jax.nn.silu(z)
    y = group_rmsnorm(y, norm_w)
    return y @ w_out, new_hist, final.astype(z.dtype)


def t5_bucket(rel):
    half = N_BUCKETS // 2
    max_exact = half // 2
    ret = jnp.where(rel > 0, half, 0)
    n = jnp.abs(rel)
    nf = jnp.maximum(n, 1).astype(jnp.float32)
    large = max_exact + (jnp.log(nf / max_exact) / math.log(MAX_DISTANCE / max_exact) * (half - max_exact)).astype(jnp.int32)
    large = jnp.minimum(large, half - 1)
    return ret + jnp.where(n < max_exact, n, large)


def rel_bias_heads(rel, table):
    bias = table[t5_bucket(rel)].astype(jnp.float32)
    q, k = rel.shape
    return jnp.transpose(bias, (2, 0, 1)).reshape(ATT_KV_HEADS, ATT_GROUP, q, k)


def sink_softmax(scores, sink):
    sink = sink.astype(jnp.float32)
    m = jnp.maximum(jnp.max(scores, -1, keepdims=True), sink)
    e = jnp.exp(scores - m)
    return e / (jnp.sum(e, -1, keepdims=True) + jnp.exp(sink - m))


def swa_prompt(q, k, v, sinks, table):
    b, s = q.shape[:2]
    nc = s // CHUNK
    nprev = WINDOW // CHUNK
    kw = WINDOW + CHUNK
    qb = q.reshape(b, nc, CHUNK, ATT_KV_HEADS, ATT_GROUP, ATT_HEAD_DIM)
    pad = jnp.zeros((b, WINDOW, ATT_KV_HEADS, ATT_HEAD_DIM), k.dtype)
    kp = jnp.concatenate([pad, k], 1).reshape(b, nc + nprev, CHUNK, ATT_KV_HEADS, ATT_HEAD_DIM)
    vp = jnp.concatenate([pad, v], 1).reshape(b, nc + nprev, CHUNK, ATT_KV_HEADS, ATT_HEAD_DIM)
    kb = jnp.concatenate([kp[:, j:j + nc] for j in range(nprev + 1)], axis=2)
    vb = jnp.concatenate([vp[:, j:j + nc] for j in range(nprev + 1)], axis=2)
    qoff = jnp.arange(CHUNK)
    koff = jnp.arange(kw) - WINDOW
    bias = rel_bias_heads(koff[None, :] - qoff[:, None], table)
    valid = (jnp.arange(nc)[:, None] * CHUNK + koff[None, :]) >= 0
    scores = jnp.einsum('bcqhgd,bckhd->bchgqk', qb, kb).astype(jnp.float32) * (ATT_HEAD_DIM ** -0.5) + bias
    scores = jnp.where(valid[None, :, None, None, None, :], scores, -1e30)
    probs = sink_softmax(scores, sinks.reshape(ATT_KV_HEADS, ATT_GROUP, 1, 1))
    out = jnp.einsum('bchgqk,bckhd->bcqhgd', probs.astype(v.dtype), vb)
    return out.reshape(b, s, ATT_HEADS * ATT_HEAD_DIM)


def swa_sample(q, k_new, v_new, cache_k, cache_v, sinks, table):
    b, sd = q.shape[:2]
    kall = jnp.concatenate([cache_k, k_new], 1)
    vall = jnp.concatenate([cache_v, v_new], 1)
    qoff = jnp.arange(sd)
    koff = jnp.concatenate([jnp.arange(WINDOW) - WINDOW, jnp.arange(sd)])
    bias = rel_bias_heads(koff[None, :] - qoff[:, None], table)
    qg = q.reshape(b, sd, ATT_KV_HEADS, ATT_GROUP, ATT_HEAD_DIM)
    scores = jnp.einsum('bqhgd,bkhd->bhgqk', qg, kall).astype(jnp.float32) * (ATT_HEAD_DIM ** -0.5) + bias
    probs = sink_softmax(scores, sinks.reshape(ATT_KV_HEADS, ATT_GROUP, 1, 1))
    out = jnp.einsum('bhgqk,bkhd->bqhgd', probs.astype(vall.dtype), vall)
    return out.reshape(b, sd, ATT_HEADS * ATT_HEAD_DIM)


def clamped_swiglu(h):
    gate = jnp.minimum(h[..., :D_FF], SWIGLU_LIMIT)
    up = jnp.clip(h[..., D_FF:], -SWIGLU_LIMIT, SWIGLU_LIMIT)
    return (up + 1.0) * gate * jax.nn.sigmoid(SWIGLU_ALPHA * gate)


def moe(x, router_w, router_b, w_up, b_up, w_down, b_down):
    b, s, d = x.shape
    t = b * s
    xt = x.reshape(t, d)
    logits = (xt @ router_w).astype(jnp.float32) + router_b.astype(jnp.float32)
    top_v, top_i = lax.top_k(logits, TOP_K)
    gates = jax.nn.softmax(top_v, axis=-1).astype(x.dtype)
    n_assign = t * TOP_K
    flat_e = top_i.reshape(-1)
    order = jnp.argsort(flat_e)
    sorted_e = flat_e[order]
    sorted_tok = (order // TOP_K).astype(jnp.int32)
    sorted_gate = gates.reshape(-1)[order]
    counts = jnp.bincount(flat_e, length=N_EXPERTS)
    padded = (counts + MOE_BLOCK - 1) // MOE_BLOCK * MOE_BLOCK
    start = jnp.cumsum(counts) - counts
    pend = jnp.cumsum(padded)
    pstart = pend - padded
    dest = pstart[sorted_e] + jnp.arange(n_assign) - start[sorted_e]
    n_blocks = (n_assign + N_EXPERTS * (MOE_BLOCK - 1) + MOE_BLOCK - 1) // MOE_BLOCK
    rows = n_blocks * MOE_BLOCK
    row_tok = jnp.full((rows,), t, jnp.int32).at[dest].set(sorted_tok)
    row_gate = jnp.zeros((rows,), x.dtype).at[dest].set(sorted_gate)
    block_e = jnp.minimum(jnp.searchsorted(pend, jnp.arange(n_blocks) * MOE_BLOCK, side='right'), N_EXPERTS - 1)
    xpad = jnp.concatenate([xt, jnp.zeros((1, d), xt.dtype)], 0)

    def body(y, blk):
        tok, g, e = blk
        h = xpad[tok] @ w_up[e] + b_up[e]
        out = clamped_swiglu(h) @ w_down[e] + b_down[e]
        return y.at[tok].add(out * g[:, None]), None

    y0 = jnp.zeros((t + 1, d), x.dtype)
    y, _ = lax.scan(body, y0, (row_tok.reshape(n_blocks, MOE_BLOCK), row_gate.reshape(n_blocks, MOE_BLOCK), block_e))
    return y[:t].reshape(b, s, d)


def trunk_layer(x, ssd_hist, ssd_state, sc_hist, kv_cache, rel_bias,
                w_in, ssd_conv_w, ssd_conv_b, dt_bias, a_log, d_skip, ssd_norm_w, w_ssd_out,
                sinks, w_att_out, sc_w, w_sc_out, w_o, ln1_g, ln1_b,
                router_w, router_b, w_up, b_up, w_down, b_down, ln2_g, ln2_b):
    b, s, _ = x.shape
    h = x @ w_in
    (z, xbc, dt_raw, q, k, v, sc_b, sc_c, sc_h, g_ssd, g_att, g_sc) = jnp.split(h, _split_points(), axis=-1)
    k = k.reshape(b, s, ATT_KV_HEADS, ATT_HEAD_DIM)
    v = v.reshape(b, s, ATT_KV_HEADS, ATT_HEAD_DIM)
    if kv_cache is None:
        att = swa_prompt(q, k, v, sinks, rel_bias)
        new_k, new_v = k[:, -WINDOW:], v[:, -WINDOW:]
        chunk = CHUNK
    else:
        att = swa_sample(q, k, v, kv_cache[0], kv_cache[1], sinks, rel_bias)
        new_k, new_v = k, v
        chunk = s
    y_ssd, new_ssd_hist, new_ssd_state = ssd_branch(z, xbc, dt_raw, ssd_hist, ssd_state, ssd_conv_w, ssd_conv_b,
                                                    dt_bias, a_log, d_skip, ssd_norm_w, w_ssd_out, chunk)
    sc_conv, new_sc_hist = causal_conv(sc_c * sc_h, sc_hist, sc_w)
    y_sc = (sc_b * sc_conv) @ w_sc_out
    y_att = att @ w_att_out
    merged = jax.nn.sigmoid(g_ssd) * y_ssd + jax.nn.sigmoid(g_att) * y_att + jax.nn.sigmoid(g_sc) * y_sc
    x = layer_norm(ALPHA * x + merged @ w_o, ln1_g, ln1_b)
    x = layer_norm(ALPHA * x + moe(x, router_w, router_b, w_up, b_up, w_down, b_down), ln2_g, ln2_b)
    return x, new_k, new_v, new_ssd_state, new_ssd_hist, new_sc_hist


def setup_inputs(seed: int = 0) -> dict:
    key = jax.random.key(seed)
    ks = iter(jax.random.split(key, 40))
    f32 = jnp.float32

    def nrm(shape, scale):
        return jax.random.normal(next(ks), shape, f32) * scale

    v_start = SSD_D_INNER + SSD_CONV_DIM + SSD_N_HEADS + ATT_HEADS * ATT_HEAD_DIM + ATT_KV_HEADS * ATT_HEAD_DIM
    v_size = ATT_KV_HEADS * ATT_HEAD_DIM
    col_scale = jnp.concatenate([jnp.ones((v_start,), f32), jnp.full((v_size,), BETA, f32),
                                 jnp.ones((D_IN - v_start - v_size,), f32)])
    dt0 = jnp.exp(jax.random.uniform(next(ks), (DEPTH, SSD_N_HEADS), f32) * (math.log(0.1) - math.log(0.001)) + math.log(0.001))
    return {
        'x_prompt': nrm((BATCH, SEQ, D_MODEL), 1.0),
        'x_sample': nrm((DEC_BATCH, DEC_SEQ, D_MODEL), 1.0),
        'cache_attn_k': nrm((DEPTH, DEC_BATCH, WINDOW, ATT_KV_HEADS, ATT_HEAD_DIM), 1.0),
        'cache_attn_v': nrm((DEPTH, DEC_BATCH, WINDOW, ATT_KV_HEADS, ATT_HEAD_DIM), BETA),
        'state_ssd': nrm((DEPTH, DEC_BATCH, SSD_N_HEADS, SSD_HEAD_DIM, SSD_D_STATE), 0.1),
        'state_ssd_conv': nrm((DEPTH, DEC_BATCH, SSD_CONV - 1, SSD_CONV_DIM), 1.0),
        'state_short_conv': nrm((DEPTH, DEC_BATCH, SC_WIDTH - 1, SC_DIM), 1.0),
        'w_in': nrm((DEPTH, D_MODEL, D_IN), D_MODEL ** -0.5) * col_scale,
        'ssd_conv_w': nrm((DEPTH, SSD_CONV, SSD_CONV_DIM), SSD_CONV ** -0.5),
        'ssd_conv_b': nrm((DEPTH, SSD_CONV_DIM), 0.02),
        'ssd_dt_bias': dt0 + jnp.log(-jnp.expm1(-dt0)),
        'ssd_a_log': jnp.log(jax.random.uniform(next(ks), (DEPTH, SSD_N_HEADS), f32, 1.0, 16.0)),
        'ssd_d': 1.0 + nrm((DEPTH, SSD_N_HEADS), 0.1),
        'ssd_norm_w': 1.0 + nrm((DEPTH, SSD_D_INNER), 0.02),
        'w_ssd_out': nrm((DEPTH, SSD_D_INNER, D_MODEL), BETA * SSD_D_INNER ** -0.5),
        'attn_sinks': nrm((DEPTH, ATT_HEADS), 0.5),
        'w_attn_out': nrm((DEPTH, ATT_HEADS * ATT_HEAD_DIM, D_MODEL), BETA * (ATT_HEADS * ATT_HEAD_DIM) ** -0.5),
        'rel_bias': nrm((N_BUCKETS, ATT_HEADS), 0.5),
        'sc_conv_w': nrm((DEPTH, SC_WIDTH, SC_DIM), SC_WIDTH ** -0.5),
        'w_sc_out': nrm((DEPTH, SC_DIM, D_MODEL), BETA * SC_DIM ** -0.5),
        'w_o': nrm((DEPTH, D_MODEL, D_MODEL), BETA * D_MODEL ** -0.5),
        'ln1_g': 1.0 + nrm((DEPTH, D_MODEL), 0.02),
        'ln1_b': nrm((DEPTH, D_MODEL), 0.02),
        'router_w': nrm((DEPTH, D_MODEL, N_EXPERTS), D_MODEL ** -0.5),
        'router_b': nrm((DEPTH, N_EXPERTS), 0.01),
        'w_up': nrm((DEPTH, N_EXPERTS, D_MODEL, 2 * D_FF), BETA * D_MODEL ** -0.5),
        'b_up': nrm((DEPTH, N_EXPERTS, 2 * D_FF), 0.02),
        'w_down': nrm((DEPTH, N_EXPERTS, D_FF, D_MODEL), BETA * D_FF ** -0.5),
        'b_down': nrm((DEPTH, N_EXPERTS, D_MODEL), 0.02),
        'ln2_g': 1.0 + nrm((DEPTH, D_MODEL), 0.02),
        'ln2_b': nrm((DEPTH, D_MODEL), 0.02),
    }


def reference(x_prompt, x_sample, cache_attn_k, cache_attn_v, state_ssd, state_ssd_conv, state_short_conv,
              w_in, ssd_conv_w, ssd_conv_b, ssd_dt_bias, ssd_a_log, ssd_d, ssd_norm_w, w_ssd_out,
              attn_sinks, w_attn_out, rel_bias, sc_conv_w, w_sc_out, w_o, ln1_g, ln1_b,
              router_w, router_b, w_up, b_up, w_down, b_down, ln2_g, ln2_b):
    bp = x_prompt.shape[0]
    dt = x_prompt.dtype
    yp, ys = x_prompt, x_sample
    kp, vp, sp, cp, scp = [], [], [], [], []
    kd, vd, sd, cd, scd = [], [], [], [], []
    for l in range(DEPTH):
        lp = (w_in[l], ssd_conv_w[l], ssd_conv_b[l], ssd_dt_bias[l], ssd_a_log[l], ssd_d[l], ssd_norm_w[l],
              w_ssd_out[l], attn_sinks[l], w_attn_out[l], sc_conv_w[l], w_sc_out[l], w_o[l], ln1_g[l], ln1_b[l],
              router_w[l], router_b[l], w_up[l], b_up[l], w_down[l], b_down[l], ln2_g[l], ln2_b[l])
        yp, k_, v_, s_, c_, sc_ = trunk_layer(
            yp, jnp.zeros((bp, SSD_CONV - 1, SSD_CONV_DIM), dt),
            jnp.zeros((bp, SSD_N_HEADS, SSD_HEAD_DIM, SSD_D_STATE), dt),
            jnp.zeros((bp, SC_WIDTH - 1, SC_DIM), dt), None, rel_bias, *lp)
        kp.append(k_); vp.append(v_); sp.append(s_); cp.append(c_); scp.append(sc_)
        ys, k_, v_, s_, c_, sc_ = trunk_layer(
            ys, state_ssd_conv[l], state_ssd[l], state_short_conv[l],
            (cache_attn_k[l], cache_attn_v[l]), rel_bias, *lp)
        kd.append(k_); vd.append(v_); sd.append(s_); cd.append(c_); scd.append(sc_)
    return (yp, ys,
            jnp.stack(kp), jnp.stack(vp), jnp.stack(sp), jnp.stack(cp), jnp.stack(scp),
            jnp.stack(kd), jnp.stack(vd), jnp.stack(sd), jnp.stack(cd), jnp.stack(scd))
```

```python
import math
import os
from contextlib import ExitStack
import numpy as np
import concourse.bass as bass
import concourse.mybir as mybir
from concourse.bass_utils import run_bass_kernel_spmd

F32 = mybir.dt.float32
BF16 = mybir.dt.bfloat16
AF = mybir.ActivationFunctionType
ALU = mybir.AluOpType
AX = mybir.AxisListType

N_CORES = 8
D = 1024
SEQ = 4096
DEC = 16
ALPHA = (2.0 * 2) ** 0.25
LN_EPS = 1e-5
NEG = -30000.0

C_Z, C_XBC, C_GSSD, C_Q, C_KD, C_GATT, C_SCC, C_SCH, C_SCB, C_GSC, C_KV, DINR = (
    0, 2048, 6144, 7168, 8192, 8704, 9728, 10752, 11776, 12800, 13824, 14336)
P_CW, P_CB, P_NW, P_SCW, P_L1G, P_L1B, P_L2G, P_L2B, P_BUP, NPP = 0, 128, 160, 176, 200, 208, 216, 224, 232, 744
B_DTB, B_ALOG, B_DD, B_SINK, B_RB, NBC = 0, 32, 64, 96, 112, 144

NSLOT = 4
ASTOP_G = int(os.environ.get('ASTOP', '99'))


class Sched:
    def __init__(self, nc):
        self.nc = nc
        self.eng = {"pe": nc.tensor, "dve": nc.vector, "act": nc.scalar, "pool": nc.gpsimd, "sp": nc.sync}
        self.sem = {}
        self.cnt = {}
        self._ctx = []
        for k in self.eng:
            cm = nc.semaphore("s_" + k)
            s = cm.__enter__()
            self._ctx.append(cm)
            self.sem[k] = s
            self.cnt[k] = 0
        self.waited = {k: {} for k in self.eng}
        self.semobj = {k: self.sem[k] for k in self.eng}
        self.tok_w = {}
        self.tok_r = {}
        self.dma_sems = {}
        self.ninst = 0

    def dma_sem(self, name):
        if name not in self.dma_sems:
            cm = self.nc.semaphore("d_" + name)
            s = cm.__enter__()
            self._ctx.append(cm)
            self.dma_sems[name] = [s, 0]
            self.semobj["d_" + name] = s
        return self.dma_sems[name]

    def _wait(self, e, semkey, val):
        if self.waited[e].get(semkey, 0) >= val:
            return
        self.eng[e].wait_ge(self.semobj[semkey], val)
        self.waited[e][semkey] = val

    def _deps(self, e, reads, writes):
        for t in reads:
            for sk, v in self.tok_w.get(t, {}).items():
                if sk == "pe" and e == "pe":
                    continue
                self._wait(e, sk, v)
        for t in writes:
            for sk, v in self.tok_w.get(t, {}).items():
                if sk == e:
                    continue
                self._wait(e, sk, v)
            for sk, v in self.tok_r.get(t, {}).items():
                if sk == e:
                    continue
                self._wait(e, sk, v)

    def _mark(self, semkey, val, reads, writes):
        for t in reads:
            self.tok_r.setdefault(t, {})[semkey] = val
        for t in writes:
            self.tok_w[t] = {semkey: val}
            self.tok_r[t] = {}

    def op(self, e, fn, reads=(), writes=()):
        self._deps(e, reads, writes)
        ins = fn(self.eng[e])
        self.cnt[e] += 1
        ins.then_inc(self.sem[e], 1)
        self._mark(e, self.cnt[e], reads, writes)
        self.ninst += 1

    def dma(self, q, semname, out, in_, reads=(), writes=(), **kw):
        self._deps(q, reads, writes)
        s = self.dma_sem(semname)
        ins = self.eng[q].dma_start(out=out, in_=in_, **kw)
        s[1] += 16
        ins.then_inc(s[0], 16)
        self._mark("d_" + semname, s[1], reads, writes)
        self.ninst += 1

    def barrier(self):
        for e in self.eng:
            for k in self.eng:
                if k != e and self.cnt[k] > 0:
                    self._wait(e, k, self.cnt[k])
            for name, (s, c) in self.dma_sems.items():
                if c > 0:
                    self._wait(e, "d_" + name, c)


def t5_bucket_np(rel):
    half, max_exact = 16, 8
    ret = np.where(rel > 0, half, 0)
    n = np.abs(rel)
    nf = np.maximum(n, 1).astype(np.float32)
    large = max_exact + (np.log(nf / np.float32(max_exact)) / np.float32(math.log(128 / max_exact))
                         * np.float32(half - max_exact)).astype(np.int32)
    large = np.minimum(large, half - 1)
    return ret + np.where(n < max_exact, n, large)


def make_consts():
    t = np.arange(128)
    same = (t[:, None] // 64) == (t[None, :] // 64)
    U = (same & (t[:, None] <= t[None, :])).astype(np.float32)
    Ls = (same & (t[:, None] > t[None, :])).astype(np.float32)
    SC = same.astype(np.float32)
    CI0 = np.repeat((t < 64).astype(np.float32)[:, None], 128, 1)
    CI1 = np.repeat((t >= 64).astype(np.float32)[:, None], 128, 1)
    msk = np.stack([U, Ls, SC, CI0, CI1], 1)
    j = np.arange(256)
    qa = t[:, None] < 64
    valid = np.where(qa, j[None, :] < 192, j[None, :] >= 64)
    bmmask = np.where(valid, 0.0, NEG).astype(np.float32)
    rel = np.arange(383) - 255
    bk = t5_bucket_np(rel)
    ohd = np.zeros((32, 383), np.float32)
    ohd[bk, np.arange(383)] = 1.0
    return dict(msk=np.ascontiguousarray(msk), id32=np.eye(128, dtype=np.float32),
                ones32=np.ones((128, 128), np.float32), bmmask=bmmask, ohd=ohd)


def build_program(n_pgroups=8, do_sample=True, n_layers=2, stop=99):
    nc = bass.Bass("TRN2", target_bir_lowering=False)
    L = n_layers

    def din(name, shape):
        return nc.dram_tensor(name, list(shape), F32, kind="ExternalInput").ap()

    def dout(name, shape):
        return nc.dram_tensor(name, list(shape), F32, kind="ExternalOutput").ap()

    xp = din("xp", [2, SEQ, D]); xs_in = din("xs", [2, DEC, D])
    ckd = din("ckd", [2, 2, 128, 4, 128]); cv = din("cv", [2, 2, 128, 256])
    sst = din("sst", [2, 2, 2048, 128]); sconv = din("sconv", [2, 2, 128, 32, 3]); ssc = din("ssc", [2, 2, 128, 8, 2])
    win = din("win", [2, 128, 8, DINR]); wdt_d = din("wdt", [2, 128, 8, 32])
    wso = din("wso", [2, 128, 16, 1024]); wao = din("wao", [2, 128, 8, 1024])
    wsc = din("wsc", [2, 128, 8, 1024]); wo = din("wo", [2, 128, 8, 1024])
    wup = din("wup", [2, 32, 128, 8, 2048]); wdn = din("wdn", [2, 32, 128, 8, 1024])
    wr_d = din("wr", [2, 128, 8, 32])
    pp_d = din("pp", [128, 2, NPP]); bc_d = din("bc", [128, 2, NBC]); bdn_d = din("bdn", [32, 2, 1024])
    msk_d = din("msk", [128, 5, 128]); id32_d = din("id32", [128, 128]); ones_d = din("ones32", [128, 128])
    bmmask_d = din("bmmask", [128, 256]); ohd_d = din("ohd", [32, 383]); tab_d = din("tab", [32, 16])

    yp = dout("yp", [2, SEQ, D]); ys = dout("ys", [2, DEC, D])
    kp = dout("kp", [2, 2, 128, 256]); vp = dout("vp", [2, 2, 128, 256])
    stp = dout("stp", [2, 2, 2048, 128]); cvp = dout("cvp", [2, 2, 128, 32, 3]); scp = dout("scp", [2, 2, 128, 8, 2])
    ks = dout("ks", [2, 2, DEC, 256]); vs = dout("vs", [2, 2, DEC, 256])
    sts = dout("sts", [2, 2, 2048, 128]); cvs = dout("cvs", [2, 2, 128, 32, 3]); scs = dout("scs", [2, 2, 128, 8, 2])
    scr = nc.dram_tensor("scr_tb", [16, 383], F32, kind="Internal").ap()

    def scr_t(name, shape):
        return nc.dram_tensor(name, list(shape), BF16, kind="Internal").ap()

    s_win = scr_t("s_win", [2, 28, 128, 4096]); s_wso = scr_t("s_wso", [2, 4, 128, 4096])
    s_wao = scr_t("s_wao", [2, 2, 128, 4096]); s_wsc = scr_t("s_wsc", [2, 2, 128, 4096])
    s_wo = scr_t("s_wo", [2, 2, 128, 4096])
    s_wup = scr_t("s_wup", [2, 32, 4, 128, 4096]); s_wdn = scr_t("s_wdn", [2, 32, 2, 128, 4096])

    S = Sched(nc)
    top = ExitStack()

    uid = [0]

    def sb(stack, name, shape, dtype):
        uid[0] += 1
        return stack.enter_context(nc.sbuf_tensor("sb%d_%s" % (uid[0], name), list(shape), dtype))

    ps = top.enter_context(nc.psum_tensor("ps", [128, 8, 512], F32))
    psn = [0]

    def PSget(n=1):
        if psn[0] + n > 8:
            psn[0] = 0
        b = psn[0]
        psn[0] = (psn[0] + n) % 8
        return b

    def pst(b, n=1):
        return [("ps", b + i) for i in range(n)]

    def psb(b):
        return ps[:, b, :].bitcast(BF16)

    ring = sb(top, "ring", [128, NSLOT, 4096], BF16)
    resT = sb(top, "resT", [128, 8, 512], F32)
    xT = sb(top, "xT", [128, 8, 512], BF16)
    Sst = sb(top, "Sst", [128, 2, 2048], F32)
    Sbf = sb(top, "Sbf", [128, 2048], BF16)
    BM = sb(top, "BM", [128, 16, 256], F32)
    msk = sb(top, "msk", [128, 5, 128], F32)
    id32 = sb(top, "id32", [128, 128], F32)
    idb = sb(top, "idb", [128, 128], BF16)
    ones32 = sb(top, "ones32", [128, 128], F32)
    pp = sb(top, "pp", [128, 2, NPP], F32)
    bc = sb(top, "bc", [128, 2, NBC], F32)
    bdn = sb(top, "bdn", [32, 2, 1024], F32)
    wdt = sb(top, "wdt", [128, 2, 8, 32], F32)
    wr = sb(top, "wr", [128, 2, 8, 32], F32)
    Abc = sb(top, "Abc", [128, 2, 32], F32)
    bu7 = sb(top, "bu7", [128, 2, 32, 8], F32)
    histx = sb(top, "histx", [128, 2, 32, 2, 3], F32)
    hists = sb(top, "hists", [128, 2, 8, 2, 2], F32)
    kprev = sb(top, "kprev", [128, 2, 4, 128], BF16)
    vprev = sb(top, "vprev", [128, 2, 256], BF16)

    rslot = [0]

    def wload(tile_ap, kdim=8):
        s = rslot[0]
        rslot[0] = (s + 1) % NSLOT
        S.dma("sp", "ring%d" % s, ring[:, s, :], tile_ap, writes=[("ring", s)])
        return s

    cvn = [0]

    def conv(dst_tile, src, kdim):
        i = cvn[0]
        cvn[0] += 1
        S.dma("pool", "cv%d" % (i % 8), dst_tile.rearrange("p (k c) -> p k c", k=kdim), src)

    def ring8(s):
        return ring[:, s, :].rearrange("p (k c) -> p k c", k=8)

    def ring16(s):
        return ring[:, s, :].rearrange("p (k c) -> p k c", k=16)

    def mm(out, lhsT, rhs, start, stop, reads, writes):
        S.op("pe", lambda e: e.matmul(out, lhsT=lhsT, rhs=rhs, start=start, stop=stop), reads=reads, writes=writes)

    def tr(out, in_, ident, reads, writes):
        S.op("pe", lambda e: e.transpose(out=out, in_=in_, identity=ident), reads=reads, writes=writes)

    def act(out, in_, func, reads, writes, **kw):
        S.op("act", lambda e: e.activation(out=out, in_=in_, func=func, **kw), reads=reads, writes=writes)

    def tt(out, in0, in1, op, reads, writes, eng="dve"):
        S.op(eng, lambda e: e.tensor_tensor(out=out, in0=in0, in1=in1, op=op), reads=reads, writes=writes)

    def ts(out, in0, s1, s2, op0, op1, reads, writes, eng="dve"):
        if op1 is None:
            S.op(eng, lambda e: e.tensor_scalar(out=out, in0=in0, scalar1=s1, scalar2=None, op0=op0),
                 reads=reads, writes=writes)
        else:
            S.op(eng, lambda e: e.tensor_scalar(out=out, in0=in0, scalar1=s1, scalar2=s2, op0=op0, op1=op1),
                 reads=reads, writes=writes)

    def stt(out, in0, scalar, in1, op0, op1, reads, writes):
        S.op("dve", lambda e: e.scalar_tensor_tensor(out=out, in0=in0, scalar=scalar, in1=in1, op0=op0, op1=op1),
             reads=reads, writes=writes)

    def cp(eng, out, in_, reads, writes):
        if eng == "act":
            act(out, in_, AF.Copy, reads, writes)
        else:
            S.op(eng, lambda e: e.tensor_copy(out=out, in_=in_), reads=reads, writes=writes)

    def bcast(ap, axis, shape):
        return ap.unsqueeze(axis).broadcast_to(list(shape))

    S.dma("sp", "c_msk", msk[:], msk_d[:, :, :], writes=["msk"])
    S.dma("sp", "c_id", id32[:], id32_d[:, :], writes=["id32"])
    S.dma("sp", "c_ones", ones32[:], ones_d[:, :], writes=["ones32"])
    S.dma("sp", "c_pp", pp[:], pp_d[:, :, :], writes=["pp"])
    S.dma("sp", "c_bc", bc[:], bc_d[:, :, :], writes=["bc"])
    S.dma("sp", "c_bdn", bdn[:], bdn_d[:, :, :], writes=["bdn"])
    S.dma("sp", "c_wdt", wdt[:], wdt_d.rearrange("l p k c -> p l k c"), writes=["wdt"])
    S.dma("sp", "c_wr", wr[:], wr_d.rearrange("l p k c -> p l k c"), writes=["wr"])
    cp("dve", idb[:], id32[:], ["id32"], ["idb"])
    act(Abc[:], bc[:, :, B_ALOG:B_ALOG + 32], AF.Exp, ["bc"], ["Abc"])
    ts(Abc[:], Abc[:], -1.0, None, ALU.mult, None, ["Abc"], ["Abc"])
    for l in range(2):
        ts(bu7[:, l, :, :], pp[:, l, P_BUP:P_BUP + 512].rearrange("p (e f) -> p e f", f=16)[:, :, 8:16],
           7.0, None, ALU.add, None, ["pp"], ["bu7"])
    with ExitStack() as st:
        ohd = sb(st, "ohd", [32, 383], F32)
        tab = sb(st, "tab", [32, 16], F32)
        tbs = sb(st, "tbs", [16, 383], F32)
        bmm = sb(st, "bmm", [128, 256], F32)
        S.dma("sp", "c_ohd", ohd[:], ohd_d[:, :], writes=["ohd"])
        S.dma("sp", "c_tab", tab[:], tab_d[:, :], writes=["tab"])
        S.dma("sp", "c_bmm", bmm[:], bmmask_d[:, :], writes=["bmm"])
        b = PSget()
        mm(ps[:16, b, 0:383], tab[:, :], ohd[:, :], True, True, ["tab", "ohd"], pst(b))
        cp("dve", tbs[:], ps[:16, b, 0:383], pst(b), ["tbs"])
        S.dma("sp", "c_scr", scr[:, :], tbs[:], reads=["tbs"], writes=["scr"])
        for q in range(128):
            S.dma("sp", "c_BM", BM[q:q + 1, :, :], scr[:, 127 - q:127 - q + 256].unsqueeze(0),
                  reads=["scr"], writes=["BM"])
        tt(BM[:], BM[:], bcast(bmm[:], 1, [128, 16, 256]), ALU.add, ["BM", "bmm"], ["BM"])
        S.barrier()

    for l in range(L):
        for t in range(28):
            conv(s_win[l, t], win[l, :, :, t * 512:(t + 1) * 512], 8)
        for t in range(4):
            conv(s_wso[l, t], wso[l, :, :, t * 256:(t + 1) * 256], 16)
        for t in range(2):
            conv(s_wao[l, t], wao[l, :, :, t * 512:(t + 1) * 512], 8)
            conv(s_wsc[l, t], wsc[l, :, :, t * 512:(t + 1) * 512], 8)
            conv(s_wo[l, t], wo[l, :, :, t * 512:(t + 1) * 512], 8)
        for e_ in range(32):
            for j in range(4):
                conv(s_wup[l, e_, j], wup[l, e_, :, :, j * 512:(j + 1) * 512], 8)
            for j in range(2):
                conv(s_wdn[l, e_, j], wdn[l, e_, :, :, j * 512:(j + 1) * 512], 8)
    S.barrier()

    def load_x(G):
        nt, bt, nb = G["nt"], G["bt"], G["nb"]
        with ExitStack() as st:
            xin = [sb(st, "xin%d" % i, [128, 1024], F32) for i in range(2)]
            for bi, blk in enumerate(G["blocks"]):
                xi = xin[bi % 2]
                src = xs_in[blk["seq"], :, :] if G["sample"] else xp[blk["seq"], blk["row0"]:blk["row0"] + bt, :]
                S.dma("sp", "xin%d" % (bi % 2), xi[:bt, :], src, writes=["xin%d" % (bi % 2)])
                for half in range(2):
                    b = PSget()
                    for kk in range(4):
                        k = half * 4 + kk
                        tr(ps[:, b, kk * 128:kk * 128 + bt], xi[:bt, k * 128:(k + 1) * 128], id32[:bt, :bt],
                           ["xin%d" % (bi % 2), "id32"], pst(b))
                    src_ps = ps[:, b, :].rearrange("p (k t) -> p k t", k=4)[:, :, :bt]
                    cp("act", resT[:, half * 4:half * 4 + 4, bi * bt:(bi + 1) * bt], src_ps, pst(b),
                       [("resT", half * 4 + kk) for kk in range(4)])
                    cp("act", xT[:, half * 4:half * 4 + 4, bi * bt:(bi + 1) * bt], src_ps, pst(b), ["xT"])
            S.barrier()

    def proj_fm(slot, c0, nt, src=None, srctok="xT", kdim=8):
        b = PSget()
        rv = ring8(slot) if kdim == 8 else ring16(slot)
        xsrc = xT if src is None else src
        for k in range(kdim):
            mm(ps[:, b, :nt], rv[:, k, c0:c0 + 128], xsrc[:, k, :nt], k == 0, k == kdim - 1,
               [("ring", slot), srctok], pst(b))
        return b

    def wo_accumulate(G, l, gated, first):
        nt = G["nt"]
        for t in range(2):
            slot = wload(s_wo[l, t])
            for cc in range(4):
                c = 4 * t + cc
                b = proj_fm(slot, cc * 128, nt, src=gated, srctok="gated")
                if first:
                    stt(resT[:, c, :nt], resT[:, c, :nt], ALPHA, ps[:, b, :nt], ALU.mult, ALU.add,
                        [("resT", c)] + pst(b), [("resT", c)])
                else:
                    tt(resT[:, c, :nt], ps[:, b, :nt], resT[:, c, :nt], ALU.add, [("resT", c)] + pst(b),
                       [("resT", c)])

    def gate_and_out(G, l, st, gcol, wsrc_fn, ntile, kdim, ysrc, ysrctok, first):
        nt = G["nt"]
        gate = sb(st, "gate", [128, 8, nt], BF16)
        gated = sb(st, "gated", [128, 8, nt], BF16)
        for t in range(2):
            slot = wload(s_win[l, gcol // 512 + t])
            for cc in range(4):
                c = 4 * t + cc
                b = proj_fm(slot, cc * 128, nt)
                act(gate[:, c, :nt], ps[:, b, :nt], AF.Sigmoid, pst(b), ["gate"])
        per = 8 // ntile
        for t in range(ntile):
            slot = wload(wsrc_fn(t))
            for cc in range(per):
                c = per * t + cc
                b = proj_fm(slot, cc * 128, nt, src=ysrc, srctok=ysrctok, kdim=kdim)
                tt(gated[:, c, :nt], ps[:, b, :nt], gate[:, c, :nt], ALU.mult, pst(b) + ["gate"], ["gated"])
        wo_accumulate(G, l, gated, first)

    def phase_ssd(G, l):
        nt, bt, nb, nseg, segt = G["nt"], G["bt"], G["nb"], G["nseg"], G["segt"]
        sample = G["sample"]
        with ExitStack() as st:
            zs = sb(st, "zs", [128, nb, 2048], BF16)
            xc = sb(st, "xc", [128, 32, nt], BF16)
            ynT = sb(st, "ynT", [128, 16, nt], BF16)
            dtt = sb(st, "dtt", [128, nb, 32], F32)
            adt = sb(st, "adt", [128, nb, 32], F32)
            for zt in range(4):
                slot = wload(s_win[l, C_Z // 512 + zt])
                for bi in range(nb):
                    b = PSget()
                    for k in range(8):
                        mm(ps[:bt, b, :], xT[:, k, bi * bt:(bi + 1) * bt], ring8(slot)[:, k, :], k == 0, k == 7,
                           ["xT", ("ring", slot)], pst(b))
                    act(zs[:bt, bi, zt * 512:(zt + 1) * 512], ps[:bt, b, :], AF.Silu, pst(b), [("zs", bi)])
            with ExitStack() as st2:
                raw = [sb(st2, "raw%d" % i, [128, nseg, 3 + segt], F32) for i in range(2)]
                cacc = [sb(st2, "cacc%d" % i, [128, nseg, segt], F32) for i in range(2)]
                for t in range(8):
                    slot = wload(s_win[l, C_XBC // 512 + t])
                    for cc in range(4):
                        c = 4 * t + cc
                        r, a = raw[c % 2], cacc[c % 2]
                        rt, at = "raw%d" % (c % 2), "cacc%d" % (c % 2)
                        b = proj_fm(slot, cc * 128, nt)
                        cp("act", r[:, :, 3:3 + segt], ps[:, b, :nt].rearrange("p (s t) -> p s t", s=nseg),
                           pst(b), [rt])
                        cp("dve", r[:, :, 0:3], histx[:, l, c, :nseg, :], [("histx", l)], [rt])
                        cp("dve", histx[:, l, c, :nseg, :], r[:, :, segt:segt + 3], [rt], [("histx", l)])
                        ts(a[:], r[:, :, 0:segt], pp[:, l, P_CW + 4 * c:P_CW + 4 * c + 1], None, ALU.mult, None,
                           [rt, "pp"], [at])
                        for j in range(1, 4):
                            stt(a[:], r[:, :, j:j + segt], pp[:, l, P_CW + 4 * c + j:P_CW + 4 * c + j + 1], a[:],
                                ALU.mult, ALU.add, [rt, at, "pp"], [at])
                        act(xc[:, c, :nt].rearrange("p (s t) -> p s t", s=nseg), a[:], AF.Silu, [at, "pp"],
                            [("xc", c)], bias=pp[:, l, P_CB + c:P_CB + c + 1], scale=1.0)
                for si in range(nseg):
                    blk = G["blocks"][si if sample else nb - 1]
                    if blk["last"]:
                        dst = (cvs if sample else cvp)[l, blk["seq"], :, :, :]
                        S.dma("sp", "o_cv", dst, histx[:, l, :, si, :], reads=[("histx", l)])
                tmpa = sb(st2, "tmpa", [128, 32], F32)
                tmpb = sb(st2, "tmpb", [128, 32], F32)
                for bi in range(nb):
                    b = PSget()
                    for k in range(8):
                        mm(ps[:bt, b, 0:32], resT[:, k, bi * bt:(bi + 1) * bt], wdt[:, l, k, :], k == 0, k == 7,
                           [("resT", k), "wdt"], pst(b))
                    tt(tmpa[:bt, :], ps[:bt, b, 0:32], bc[:bt, l, B_DTB:B_DTB + 32], ALU.add, pst(b) + ["bc"],
                       ["tmpa"])
                    act(tmpb[:bt, :], tmpa[:bt, :], AF.Exp, ["tmpa"], ["tmpb"])
                    ts(tmpa[:bt, :], tmpb[:bt, :], 1.0, None, ALU.add, None, ["tmpb"], ["tmpa"])
                    act(dtt[:bt, bi, :], tmpa[:bt, :], AF.Ln, ["tmpa"], ["dtt"])
                    tt(adt[:bt, bi, :], dtt[:bt, bi, :], Abc[:bt, l, :], ALU.mult, ["dtt", "Abc"], ["adt"])
                S.barrier()
            with ExitStack() as st2:
                acs = sb(st2, "acs", [128, 32], F32)
                ea = sb(st2, "ea", [128, 32], F32)
                d2e = sb(st2, "d2e", [128, 32], F32)
                dtd = sb(st2, "dtd", [128, 2, 32], F32)
                tmpd = sb(st2, "tmpd", [128, 32], F32)
                cdec = sb(st2, "cdec", [128, 2, 32], F32)
                Btok = sb(st2, "Btok", [128, 8, 128], BF16)
                CTm = sb(st2, "CTm", [128, 2, 8, 128], BF16)
                stg = sb(st2, "stg", [128, 4, 128], F32)
                xs_t = sb(st2, "xs_t", [128, 256], BF16)
                xsD = sb(st2, "xsD", [128, 256], BF16)
                xdt = sb(st2, "xdt", [128, 256], BF16)
                xdtd = sb(st2, "xdtd", [128, 2, 256], BF16)
                R = sb(st2, "R", [128, 4, bt], F32)
                E = sb(st2, "E", [128, 4, bt], F32)
                cbm = sb(st2, "cbm", [128, bt], F32)
                M = sb(st2, "M", [128, 4, bt], BF16)
                yo = sb(st2, "yo", [128, 256], F32)
                yg = sb(st2, "yg", [128, 256], F32)
                junk = sb(st2, "junk", [128, 256], F32)
                yn = sb(st2, "yn", [128, 256], BF16)
                sm = sb(st2, "sm", [128, 4], F32)
                S.op("dve", lambda e: e.memset(CTm[:], 0.0), writes=["CTm"])
                for bi, blk in enumerate(G["blocks"]):
                    cols = slice(bi * bt, (bi + 1) * bt)
                    chunks = blk["chunks"]
                    if blk["first"]:
                        if sample:
                            for c4 in range(4):
                                S.dma("sp", "stg", stg[:],
                                      sst[l, blk["seq"], c4 * 512:(c4 + 1) * 512, :].rearrange("(c p) n -> p c n", p=128),
                                      writes=["stg"])
                                b = PSget()
                                for i in range(4):
                                    tr(ps[:, b, i * 128:(i + 1) * 128], stg[:, i, :], id32[:, :], ["stg", "id32"],
                                       pst(b))
                                cp("act", Sst[:, l, c4 * 512:(c4 + 1) * 512], ps[:, b, :], pst(b),
                                   [("S", l, c4 * 2), ("S", l, c4 * 2 + 1)])
                        else:
                            S.op("dve", lambda e: e.memset(Sst[:, l, :], 0.0), writes=[("S", l, g) for g in range(8)])
                    if bi == 0 or sample:
                        cp("act", Sbf[:], Sst[:, l, :], [("S", l, g) for g in range(8)],
                           [("Sbf", g) for g in range(8)])
                    b = PSget()
                    mm(ps[:bt, b, 0:32], msk[:bt, 0, :bt], adt[:bt, bi, :], True, True, ["msk", "adt"], pst(b))
                    mm(ps[:bt, b, 32:64], msk[:bt, 2, :bt], adt[:bt, bi, :], True, True, ["msk", "adt"], pst(b))
                    cp("act", acs[:bt, :], ps[:bt, b, 0:32], pst(b), ["acs"])
                    act(ea[:bt, :], ps[:bt, b, 0:32], AF.Exp, pst(b), ["ea"])
                    tt(tmpd[:bt, :], ps[:bt, b, 32:64], acs[:bt, :], ALU.subtract, pst(b) + ["acs"], ["tmpd"])
                    act(d2e[:bt, :], tmpd[:bt, :], AF.Exp, ["tmpd"], ["d2e"])
                    tt(dtd[:bt, 0, :], dtt[:bt, bi, :], d2e[:bt, :], ALU.mult, ["dtt", "d2e"], ["dtd"])
                    if len(chunks) == 2:
                        ts(dtd[:bt, 1, :], dtd[:bt, 0, :], msk[:bt, 4, 0:1], None, ALU.mult, None, ["dtd", "msk"],
                           ["dtd"])
                        ts(dtd[:bt, 0, :], dtd[:bt, 0, :], msk[:bt, 3, 0:1], None, ALU.mult, None, ["dtd", "msk"],
                           ["dtd"])
                    for j in range(len(chunks)):
                        b2 = PSget()
                        mm(ps[:, b2, 0:32], msk[:bt, 3 + j, :], adt[:bt, bi, :], True, True, ["msk", "adt"], pst(b2))
                        act(cdec[:, j, :], ps[:, b2, 0:32], AF.Exp, pst(b2), ["cdec"])
                    b = PSget()
                    for g in range(8):
                        tr(psb(b)[:bt, g * 128:(g + 1) * 128], xc[:, 16 + g, cols], idb[:, :],
                           [("xc", 16 + g), "idb"], pst(b))
                    cp("act", Btok[:bt, :, :], psb(b)[:bt, :].rearrange("p (g n) -> p g n", g=8), pst(b), ["Btok"])
                    if len(chunks) == 2:
                        for j, (p0, ln) in enumerate(chunks):
                            cp("dve", CTm[:, j, :, p0:p0 + ln], xc[:, 24:32, bi * bt + p0:bi * bt + p0 + ln],
                               [("xc", 24 + g) for g in range(8)], ["CTm"])
                    for g in range(8):
                        h4 = slice(4 * g, 4 * g + 4)
                        b = PSget()
                        for i in range(2):
                            tr(psb(b)[:bt, i * 128:(i + 1) * 128], xc[:, 2 * g + i, cols], idb[:, :],
                               [("xc", 2 * g + i), "idb"], pst(b))
                        cp("act", xs_t[:bt, :], psb(b)[:bt, 0:256], pst(b), ["xs_t"])
                        xs3 = xs_t[:bt, :].rearrange("p (h d) -> p h d", h=4)
                        tt(xsD[:bt, :].rearrange("p (h d) -> p h d", h=4), xs3,
                           bcast(bc[:bt, l, B_DD + 4 * g:B_DD + 4 * g + 4], 2, [bt, 4, 64]), ALU.mult,
                           ["xs_t", "bc"], ["xsD"])
                        tt(xdt[:bt, :].rearrange("p (h d) -> p h d", h=4), xs3,
                           bcast(dtt[:bt, bi, h4], 2, [bt, 4, 64]), ALU.mult, ["xs_t", "dtt"], ["xdt"])
                        for j in range(len(chunks)):
                            tt(xdtd[:bt, j, :].rearrange("p (h d) -> p h d", h=4), xs3,
                               bcast(dtd[:bt, j, h4], 2, [bt, 4, 64]), ALU.mult, ["xs_t", "dtd"], ["xdtd"])
                        tt(R[:bt, :, :], bcast(msk[:bt, 0, :bt], 1, [bt, 4, bt]),
                           bcast(adt[:bt, bi, h4], 2, [bt, 4, bt]), ALU.mult, ["msk", "adt"], ["R"])
                        b = PSget()
                        mm(ps[:bt, b, 0:4 * bt], msk[:bt, 1, :bt], R[:bt, :, :].rearrange("p h l -> p (h l)"),
                           True, True, ["msk", "R"], pst(b))
                        act(E[:bt, :, :].rearrange("p h l -> p (h l)"), ps[:bt, b, 0:4 * bt], AF.Exp, pst(b), ["E"])
                        b = PSget()
                        mm(ps[:bt, b, 0:bt], xc[:, 16 + g, cols], xc[:, 24 + g, cols], True, True,
                           [("xc", 16 + g), ("xc", 24 + g)], pst(b))
                        tt(cbm[:bt, :], ps[:bt, b, 0:bt], msk[:bt, 0, :bt], ALU.mult, pst(b) + ["msk"], ["cbm"])
                        tt(M[:bt, :, :], E[:bt, :, :], bcast(cbm[:bt, :], 1, [bt, 4, bt]), ALU.mult, ["E", "cbm"],
                           ["M"])
                        byo = PSget()
                        for j, (p0, ln) in enumerate(chunks):
                            if len(chunks) == 2:
                                lc, lct = CTm[:, j, g, :bt], "CTm"
                            else:
                                lc, lct = xc[:, 24 + g, cols], ("xc", 24 + g)
                            mm(ps[:bt, byo, 0:256], lc, Sbf[:, g * 256:(g + 1) * 256], j == 0, j == len(chunks) - 1,
                               [lct, ("Sbf", g)], pst(byo))
                            bds = PSget()
                            mm(ps[:, bds, 0:256], Btok[:bt, g, :], xdtd[:bt, j, :], True, True,
                               ["Btok", "xdtd"], pst(bds))
                            Sg = Sst[:, l, g * 256:(g + 1) * 256]
                            tt(Sg.rearrange("p (h d) -> p h d", h=4), Sg.rearrange("p (h d) -> p h d", h=4),
                               bcast(cdec[:, j, h4], 2, [128, 4, 64]), ALU.mult, [("S", l, g), "cdec"], [("S", l, g)])
                            tt(Sg, ps[:, bds, 0:256], Sg, ALU.add, pst(bds) + [("S", l, g)], [("S", l, g)])
                            cp("act", Sbf[:, g * 256:(g + 1) * 256], Sg, [("S", l, g)], [("Sbf", g)])
                        tt(yo[:bt, :].rearrange("p (h d) -> p h d", h=4),
                           ps[:bt, byo, 0:256].rearrange("p (h d) -> p h d", h=4),
                           bcast(ea[:bt, h4], 2, [bt, 4, 64]), ALU.mult, pst(byo) + ["ea"], ["yo"])
                        by = PSget()
                        for hh in range(4):
                            mm(ps[:bt, by, hh * 64:(hh + 1) * 64], M[:bt, hh, :], xdt[:bt, hh * 64:(hh + 1) * 64],
                               True, False, ["M", "xdt"], pst(by))
                            mm(ps[:bt, by, hh * 64:(hh + 1) * 64], idb[:bt, :bt], xsD[:bt, hh * 64:(hh + 1) * 64],
                               False, True, ["idb", "xsD"], pst(by))
                        tt(yg[:bt, :], ps[:bt, by, 0:256], yo[:bt, :], ALU.add, pst(by) + ["yo"], ["yg"])
                        tt(yg[:bt, :], yg[:bt, :], zs[:bt, bi, g * 256:(g + 1) * 256], ALU.mult, ["yg", ("zs", bi)],
                           ["yg"])
                        act(junk[:bt, :], yg[:bt, :], AF.Square, ["yg"], ["junk", "sm"], accum_out=sm[:bt, 0:1])
                        ts(sm[:bt, 1:2], sm[:bt, 0:1], 1.0 / 256, LN_EPS, ALU.mult, ALU.add, ["sm"], ["sm"])
                        act(sm[:bt, 2:3], sm[:bt, 1:2], AF.Sqrt, ["sm"], ["sm"])
                        S.op("dve", lambda e: e.reciprocal(out=sm[:bt, 3:4], in_=sm[:bt, 2:3]), reads=["sm"],
                             writes=["sm"])
                        ts(yn[:bt, :], yg[:bt, :], sm[:bt, 3:4], None, ALU.mult, None, ["yg", "sm"], ["yn"])
                        b = PSget()
                        for i in range(2):
                            tr(psb(b)[:, i * 128:i * 128 + bt], yn[:bt, i * 128:(i + 1) * 128], idb[:bt, :bt],
                               ["yn", "idb"], pst(b))
                        tt(ynT[:, 2 * g:2 * g + 2, cols],
                           psb(b)[:, 0:256].rearrange("p (i t) -> p i t", i=2)[:, :, :bt],
                           bcast(pp[:, l, P_NW + 2 * g:P_NW + 2 * g + 2], 2, [128, 2, bt]), ALU.mult,
                           pst(b) + ["pp"], ["ynT"])
                    if blk["last"]:
                        for c4 in range(4):
                            b = PSget()
                            for i in range(4):
                                c = 4 * c4 + i
                                tr(ps[:, b, i * 128:(i + 1) * 128], Sst[:, l, c * 128:(c + 1) * 128], id32[:, :],
                                   [("S", l, c // 2), "id32"], pst(b))
                            cp("act", stg[:, :, :], ps[:, b, :].rearrange("p (i n) -> p i n", i=4),
                               pst(b), ["stg"])
                            dst = (sts if sample else stp)[l, blk["seq"], c4 * 512:(c4 + 1) * 512, :].rearrange(
                                "(c p) n -> p c n", p=128)
                            S.dma("sp", "stg", dst, stg[:], reads=["stg"])
                S.barrier()
            with ExitStack() as st2:
                gate_and_out(G, l, st2, C_GSSD, lambda t: s_wso[l, t], 4, 16, ynT, "ynT",
                             True)
                S.barrier()

    def phase_attn(G, l):
        nt, bt, nb = G["nt"], G["bt"], G["nb"]
        sample = G["sample"]
        with ExitStack() as st:
            qT = sb(st, "qT", [128, 16, nt], BF16)
            S.op("dve", lambda e: e.memset(qT[:], 0.0), writes=["qT"])
            kT = sb(st, "kT", [128, 4, nt], BF16)
            kvf = sb(st, "kvf", [128, nb, 512], F32)
            vb = sb(st, "vb", [128, nb, 256], BF16)
            attT = sb(st, "attT", [128, 8, nt], BF16)
            if ASTOP_G < 99:
                S.op("dve", lambda e: e.memset(attT[:], 0.0), writes=["attT"])
            for t in range(2):
                slot = wload(s_win[l, C_Q // 512 + t])
                for cc in range(4):
                    c = 4 * t + cc
                    b = proj_fm(slot, cc * 128, nt)
                    ts(qT[0:64, 2 * c, :nt], ps[0:64, b, :nt], 0.125, None, ALU.mult, None, pst(b), ["qT"])
                    ts(qT[64:128, 2 * c + 1, :nt], ps[64:128, b, :nt], 0.125, None, ALU.mult, None, pst(b), ["qT"])
            AS2 = int(os.environ.get("AS2", "99"))
            if AS2 >= 1:
                slot = wload(s_win[l, C_KD // 512])
                for g in range(4):
                    b = proj_fm(slot, g * 128, nt)
                    cp("act", kT[:, g, :nt], ps[:, b, :nt], pst(b), ["kT"])
            if AS2 >= 2:
                ckv = C_Z if os.environ.get("AS3") else C_KV
                slot = wload(s_win[l, ckv // 512])
            for bi in range(nb if AS2 >= 2 else 0):
                b = PSget()
                for k in range(8):
                    mm(ps[:bt, b, :], xT[:, k, bi * bt:(bi + 1) * bt], ring8(slot)[:, k, :], k == 0, k == 7,
                       ["xT", ("ring", slot)], pst(b))
                cp("act", kvf[:bt, bi, :], ps[:bt, b, :], pst(b), [("kvf", bi)])
                cp("dve", vb[:bt, bi, :], kvf[:bt, bi, 256:512], [("kvf", bi)], ["vb"])
            ASTOP = int(os.environ.get("ASTOP", "99"))
            for bi, blk in enumerate(G["blocks"]):
                if ASTOP < 2:
                    break
                if sample:
                    S.dma("sp", "o_k", ks[l, blk["seq"], :, :], kvf[:bt, bi, 0:256], reads=[("kvf", bi)])
                    S.dma("sp", "o_v", vs[l, blk["seq"], :, :], kvf[:bt, bi, 256:512], reads=[("kvf", bi)])
                elif blk["last"]:
                    S.dma("sp", "o_k", kp[l, blk["seq"], :, :], kvf[:bt, bi, 0:256], reads=[("kvf", bi)])
                    S.dma("sp", "o_v", vp[l, blk["seq"], :, :], kvf[:bt, bi, 256:512], reads=[("kvf", bi)])
            with ExitStack() as st2:
                s_sb = sb(st2, "s_sb", [128, 4, 256], F32)
                e_bf = sb(st2, "e_bf", [128, 4, 256], BF16)
                eT = sb(st2, "eT", [128, 4, 2, 128], BF16)
                att = sb(st2, "att", [128, 16, 64], BF16)
                sm = sb(st2, "asm", [128, 8, 4], F32)
                ckst = sb(st2, "ckst", [128, 4, 128], F32)
                for bi, blk in enumerate(G["blocks"]):
                    cols = slice(bi * bt, (bi + 1) * bt)
                    if ASTOP < 3:
                        break
                    if sample:
                        S.dma("sp", "ckst", ckst[:], ckd[l, blk["seq"], :, :, :], writes=["ckst"])
                        b = PSget()
                        for g in range(4):
                            tr(ps[:, b, g * 128:(g + 1) * 128], ckst[:, g, :], id32[:, :], ["ckst", "id32"], pst(b))
                        cp("act", kprev[:, l, :, :], ps[:, b, :].rearrange("p (g j) -> p g j", g=4), pst(b),
                           [("kprev", l)])
                        S.dma("pool", "vprev", vprev[:, l, :], cv[l, blk["seq"], :, :], writes=[("vprev", l)])
                    has_prev = sample or not blk["first"]
                    off = 128 if has_prev else 0
                    nk = off + bt
                    bm0 = 0 if has_prev else 128
                    for g in range(4):
                        if ASTOP < 4:
                            break
                        for hb in range(2):
                            bk = PSget()
                            pk = ps[:, bk, :].rearrange("p (h j) -> p h j", j=256)
                            for h2 in range(2):
                                hh = 2 * hb + h2
                                h = 4 * g + hh
                                if has_prev:
                                    mm(pk[:bt, h2, 0:128], qT[:, h, cols], kprev[:, l, g, :], True,
                                       True, ["qT", ("kprev", l)], pst(bk))
                                mm(pk[:bt, h2, off:off + bt], qT[:, h, cols], kT[:, g, cols],
                                   True, True, ["qT", "kT"], pst(bk))
                            tt(s_sb[:bt, 2 * hb:2 * hb + 2, :nk], pk[:bt, :, :nk],
                               BM[:bt, 4 * g + 2 * hb:4 * g + 2 * hb + 2, bm0:bm0 + nk], ALU.add,
                               pst(bk) + ["BM"], ["s_sb"])
                        if ASTOP < 5:
                            continue
                        S.op("dve", lambda e: e.tensor_reduce(out=sm[:bt, 0, :], in_=s_sb[:bt, :, :nk], axis=AX.X,
                                                              op=ALU.max), reads=["s_sb"], writes=["asm"])
                        sk = bc[:bt, l, B_SINK + 4 * g:B_SINK + 4 * g + 4]
                        tt(sm[:bt, 1, :], sm[:bt, 0, :], sk, ALU.max, ["asm", "bc"], ["asm"])
                        ts(sm[:bt, 2, :], sm[:bt, 1, :], -1.0, None, ALU.mult, None, ["asm"], ["asm"])
                        for hh in range(4):
                            act(e_bf[:bt, hh, :nk], s_sb[:bt, hh, :nk], AF.Exp, ["s_sb", "asm"], ["e_bf", "asm"],
                                bias=sm[:bt, 2, hh:hh + 1], scale=1.0, accum_out=sm[:bt, 3, hh:hh + 1])
                        tt(sm[:bt, 4, :], sk, sm[:bt, 2, :], ALU.add, ["asm", "bc"], ["asm"])
                        act(sm[:bt, 5, :], sm[:bt, 4, :], AF.Exp, ["asm"], ["asm"])
                        tt(sm[:bt, 6, :], sm[:bt, 3, :], sm[:bt, 5, :], ALU.add, ["asm"], ["asm"])
                        S.op("dve", lambda e: e.reciprocal(out=sm[:bt, 7, :], in_=sm[:bt, 6, :]), reads=["asm"],
                             writes=["asm"])
                        if ASTOP < 6:
                            continue
                        b = PSget()
                        pv = psb(b).rearrange("p (h a j) -> p h a j", h=4, a=2)
                        for hh in range(4):
                            if has_prev:
                                tr(pv[:, hh, 0, :bt], e_bf[:bt, hh, 0:128], idb[:bt, :bt], ["e_bf", "idb"], pst(b))
                            tr(pv[:bt, hh, 1, :bt], e_bf[:bt, hh, off:off + bt], idb[:bt, :bt], ["e_bf", "idb"],
                               pst(b))
                        if has_prev:
                            cp("act", eT[:, :, 0, :bt], pv[:, :, 0, :bt], pst(b), ["eT"])
                        cp("act", eT[:bt, :, 1, :bt], pv[:bt, :, 1, :bt], pst(b), ["eT"])
                        bo = PSget()
                        for hh in range(4):
                            if has_prev:
                                mm(ps[:bt, bo, hh * 64:(hh + 1) * 64], eT[:, hh, 0, :bt],
                                   vprev[:, l, g * 64:(g + 1) * 64], True, False, ["eT", ("vprev", l)], pst(bo))
                            mm(ps[:bt, bo, hh * 64:(hh + 1) * 64], eT[:bt, hh, 1, :bt],
                               vb[:bt, bi, g * 64:(g + 1) * 64], not has_prev, True, ["eT", "vb"], pst(bo))
                        tt(att[:bt, 4 * g:4 * g + 4, :], ps[:bt, bo, 0:256].rearrange("p (h d) -> p h d", h=4),
                           bcast(sm[:bt, 7, :], 2, [bt, 4, 64]), ALU.mult, pst(bo) + ["asm"], ["att"])
                    if ASTOP < 7:
                        continue
                    b = PSget()
                    a2 = att[:bt, :, :].rearrange("p h d -> p (h d)")
                    for c in range(8):
                        tr(psb(b)[:, c * 128:c * 128 + bt], a2[:, c * 128:(c + 1) * 128], idb[:bt, :bt],
                           ["att", "idb"], pst(b))
                    cp("act", attT[:, :, cols], psb(b).rearrange("p (c t) -> p c t", c=8)[:, :, :bt], pst(b),
                       ["attT"])
                    if not sample:
                        cp("dve", kprev[:, l, :, :], kT[:, :, cols], ["kT"], [("kprev", l)])
                        cp("act", vprev[:, l, :], vb[:, bi, :], ["vb"], [("vprev", l)])
                S.barrier()
            with ExitStack() as st2:
                if AS2 >= 3:
                    gate_and_out(G, l, st2, C_GATT, lambda t: s_wao[l, t], 2, 8, attT, "attT",
                                 False)
                S.barrier()

    def phase_sc(G, l):
        nt, bt, nb, nseg, segt = G["nt"], G["bt"], G["nb"], G["nseg"], G["segt"]
        sample = G["sample"]
        with ExitStack() as st:
            scv = sb(st, "scv", [128, 8, nt], BF16)
            with ExitStack() as st2:
                raw = [sb(st2, "sraw%d" % i, [128, nseg, 2 + segt], F32) for i in range(2)]
                cacc = [sb(st2, "scacc%d" % i, [128, nseg, segt], F32) for i in range(2)]
                tmpc = [sb(st2, "stmp%d" % i, [128, nt], F32) for i in range(2)]
                for half in range(2):
                    s_c = wload(s_win[l, C_SCC // 512 + half])
                    s_h = wload(s_win[l, C_SCH // 512 + half])
                    s_b = wload(s_win[l, C_SCB // 512 + half])
                    for cc in range(4):
                        c = 4 * half + cc
                        r, a, tm = raw[c % 2], cacc[c % 2], tmpc[c % 2]
                        rt, at, tmt = "sraw%d" % (c % 2), "scacc%d" % (c % 2), "stmp%d" % (c % 2)
                        b1 = proj_fm(s_c, cc * 128, nt)
                        b2 = proj_fm(s_h, cc * 128, nt)
                        b3 = proj_fm(s_b, cc * 128, nt)
                        cp("act", tm[:, :nt], ps[:, b1, :nt], pst(b1), [tmt])
                        tt(r[:, :, 2:2 + segt], tm[:, :nt].rearrange("p (s t) -> p s t", s=nseg),
                           ps[:, b2, :nt].rearrange("p (s t) -> p s t", s=nseg), ALU.mult, [tmt] + pst(b2), [rt])
                        cp("dve", r[:, :, 0:2], hists[:, l, c, :nseg, :], [("hists", l)], [rt])
                        cp("dve", hists[:, l, c, :nseg, :], r[:, :, segt:segt + 2], [rt], [("hists", l)])
                        ts(a[:], r[:, :, 0:segt], pp[:, l, P_SCW + 3 * c:P_SCW + 3 * c + 1], None, ALU.mult, None,
                           [rt, "pp"], [at])
                        for j in range(1, 3):
                            stt(a[:], r[:, :, j:j + segt], pp[:, l, P_SCW + 3 * c + j:P_SCW + 3 * c + j + 1], a[:],
                                ALU.mult, ALU.add, [rt, at, "pp"], [at])
                        tt(scv[:, c, :nt], a[:].rearrange("p s t -> p (s t)"), ps[:, b3, :nt], ALU.mult,
                           [at] + pst(b3), ["scv"])
                for si in range(nseg):
                    blk = G["blocks"][si if sample else nb - 1]
                    if blk["last"]:
                        dst = (scs if sample else scp)[l, blk["seq"], :, :, :]
                        S.dma("sp", "o_sc", dst, hists[:, l, :, si, :], reads=[("hists", l)])
                S.barrier()
            with ExitStack() as st2:
                gate_and_out(G, l, st2, C_GSC, lambda t: s_wsc[l, t], 2, 8, scv, "scv", False)
                S.barrier()

    def layer_norm(G, l, gcol, bcol):
        nt = G["nt"]
        with ExitStack() as st:
            sq = sb(st, "lnsq", [128, 8, nt], F32)
            mean = sb(st, "lnmean", [128, nt], F32)
            var = sb(st, "lnvar", [128, nt], F32)
            rstd = sb(st, "lnrstd", [128, nt], F32)
            tmp = [sb(st, "lntmp%d" % i, [128, nt], F32) for i in range(2)]
            for k in range(8):
                act(sq[:, k, :], resT[:, k, :nt], AF.Square, [("resT", k)], ["lnsq"])
            b1 = PSget()
            for k in range(8):
                mm(ps[:, b1, :nt], ones32[:, :], resT[:, k, :nt], k == 0, k == 7, ["ones32", ("resT", k)], pst(b1))
            b2 = PSget()
            for k in range(8):
                mm(ps[:, b2, :nt], ones32[:, :], sq[:, k, :], k == 0, k == 7, ["ones32", "lnsq"], pst(b2))
            act(mean[:], ps[:, b1, :nt], AF.Copy, pst(b1), ["lnmean"], scale=1.0 / D)
            tt(var[:], mean[:], mean[:], ALU.mult, ["lnmean"], ["lnvar"])
            stt(var[:], ps[:, b2, :nt], 1.0 / D, var[:], ALU.mult, ALU.subtract, pst(b2) + ["lnvar"], ["lnvar"])
            ts(var[:], var[:], LN_EPS, None, ALU.add, None, ["lnvar"], ["lnvar"])
            act(rstd[:], var[:], AF.Sqrt, ["lnvar"], ["lnrstd"])
            S.op("dve", lambda e: e.reciprocal(out=var[:], in_=rstd[:]), reads=["lnrstd"], writes=["lnvar"])
            for k in range(8):
                t_ = tmp[k % 2]
                tk = "lntmp%d" % (k % 2)
                tt(t_[:], resT[:, k, :nt], mean[:], ALU.subtract, [("resT", k), "lnmean"], [tk])
                tt(t_[:], t_[:], var[:], ALU.mult, [tk, "lnvar"], [tk])
                ts(resT[:, k, :nt], t_[:], pp[:, l, gcol + k:gcol + k + 1], pp[:, l, bcol + k:bcol + k + 1],
                   ALU.mult, ALU.add, [tk, "pp"], [("resT", k)])
                cp("act", xT[:, k, :nt], resT[:, k, :nt], [("resT", k)], ["xT"])
            S.barrier()

    def phase_moe(G, l):
        nt, bt, nb = G["nt"], G["bt"], G["nb"]
        with ExitStack() as st:
            GT = sb(st, "GT", [32, nt], F32)
            GTm = sb(st, "GTm", [32, nt], F32)
            Gbs = sb(st, "Gbs", [128, nt], F32)
            actT = sb(st, "actT", [128, 8, nt], BF16)
            lg = sb(st, "lg", [128, 32], F32)
            ex = sb(st, "ex", [128, 32], F32)
            sel = sb(st, "sel", [128, 32], F32)
            Gm = sb(st, "Gm", [128, 32], F32)
            mx8 = sb(st, "mx8", [128, 8], F32)
            rsm = sb(st, "rsm", [128, 4], F32)
            gbuf = [sb(st, "mg%d" % i, [128, nt], F32) for i in range(2)]
            sgbuf = [sb(st, "msg%d" % i, [128, nt], F32) for i in range(2)]
            rbuf = [sb(st, "mr%d" % i, [128, nt], F32) for i in range(2)]
            for bi in range(nb):
                cols = slice(bi * bt, (bi + 1) * bt)
                b = PSget()
                for k in range(8):
                    mm(ps[:bt, b, 0:32], resT[:, k, cols], wr[:, l, k, :], k == 0, k == 7, [("resT", k), "wr"], pst(b))
                tt(lg[:bt, :], ps[:bt, b, 0:32], bc[:bt, l, B_RB:B_RB + 32], ALU.add, pst(b) + ["bc"], ["lg"])
                S.op("dve", lambda e: e.max(out=mx8[:bt, :], in_=lg[:bt, :]), reads=["lg"], writes=["mx8"])
                ts(sel[:bt, :], lg[:bt, :], mx8[:bt, 3:4], None, ALU.is_ge, None, ["lg", "mx8"], ["sel"])
                ts(rsm[:bt, 0:1], mx8[:bt, 0:1], -1.0, None, ALU.mult, None, ["mx8"], ["rsm"])
                act(ex[:bt, :], lg[:bt, :], AF.Exp, ["lg", "rsm"], ["ex"], bias=rsm[:bt, 0:1], scale=1.0)
                tt(ex[:bt, :], ex[:bt, :], sel[:bt, :], ALU.mult, ["ex", "sel"], ["ex"])
                S.op("dve", lambda e: e.tensor_reduce(out=rsm[:bt, 1:2], in_=ex[:bt, :], axis=AX.X, op=ALU.add),
                     reads=["ex"], writes=["rsm"])
                S.op("dve", lambda e: e.reciprocal(out=rsm[:bt, 2:3], in_=rsm[:bt, 1:2]), reads=["rsm"],
                     writes=["rsm"])
                ts(Gm[:bt, :], ex[:bt, :], rsm[:bt, 2:3], None, ALU.mult, None, ["ex", "rsm"], ["Gm"])
                b = PSget()
                tr(ps[:32, b, 0:bt], Gm[:bt, :], id32[:bt, :bt], ["Gm", "id32"], pst(b))
                cp("act", GT[:, cols], ps[:32, b, 0:bt], pst(b), ["GT"])
            for c in range(8):
                b = PSget()
                mm(ps[:, b, :nt], bdn[:, l, c * 128:(c + 1) * 128], GT[:, :nt], True, True, ["bdn", "GT"], pst(b))
                stt(resT[:, c, :nt], resT[:, c, :nt], ALPHA, ps[:, b, :nt], ALU.mult, ALU.add,
                    [("resT", c)] + pst(b), [("resT", c)])
            for e_ in range(32):
                ts(GTm[:, :nt], GT[:, :nt], id32[:32, e_:e_ + 1], None, ALU.mult, None, ["GT", "id32"], ["GTm"])
                b = PSget()
                mm(ps[:, b, :nt], ones32[:32, :], GTm[:, :nt], True, True, ["ones32", "GTm"], pst(b))
                cp("act", Gbs[:, :nt], ps[:, b, :nt], pst(b), ["Gbs"])
                for j in range(4):
                    slot = wload(s_wup[l, e_, j])
                    for i in range(2):
                        fi = 2 * j + i
                        gb, sg, rb = gbuf[fi % 2], sgbuf[fi % 2], rbuf[fi % 2]
                        gt_, sgt, rbt = "mg%d" % (fi % 2), "msg%d" % (fi % 2), "mr%d" % (fi % 2)
                        bg = proj_fm(slot, i * 128, nt)
                        bu = proj_fm(slot, 256 + i * 128, nt)
                        pcol = P_BUP + 16 * e_ + fi
                        ts(gb[:, :nt], ps[:, bg, :nt], pp[:, l, pcol:pcol + 1], 7.0, ALU.add, ALU.min,
                           pst(bg) + ["pp"], [gt_])
                        act(sg[:, :nt], gb[:, :nt], AF.Sigmoid, [gt_], [sgt], scale=1.702)
                        tt(gb[:, :nt], gb[:, :nt], sg[:, :nt], ALU.mult, [gt_, sgt], [gt_])
                        act(rb[:, :nt], ps[:, bu, :nt], AF.Relu, pst(bu) + ["bu7"], [rbt],
                            bias=bu7[:, l, e_, fi:fi + 1], scale=1.0)
                        ts(rb[:, :nt], rb[:, :nt], 14.0, -6.0, ALU.min, ALU.add, [rbt], [rbt])
                        tt(rb[:, :nt], rb[:, :nt], gb[:, :nt], ALU.mult, [rbt, gt_], [rbt])
                        tt(actT[:, fi, :nt], rb[:, :nt], Gbs[:, :nt], ALU.mult, [rbt, "Gbs"], [("actT", fi)])
                for j in range(2):
                    slot = wload(s_wdn[l, e_, j])
                    for cc in range(4):
                        c = 4 * j + cc
                        b = PSget()
                        for fi in range(8):
                            mm(ps[:, b, :nt], ring8(slot)[:, fi, cc * 128:(cc + 1) * 128], actT[:, fi, :nt], fi == 0,
                               fi == 7, [("ring", slot), ("actT", fi)], pst(b))
                        tt(resT[:, c, :nt], ps[:, b, :nt], resT[:, c, :nt], ALU.add, pst(b) + [("resT", c)],
                           [("resT", c)])
            S.barrier()

    def store_y(G):
        nt, bt, nb = G["nt"], G["bt"], G["nb"]
        with ExitStack() as st:
            yst = [sb(st, "yst%d" % i, [128, 1024], F32) for i in range(2)]
            for bi, blk in enumerate(G["blocks"]):
                cols = slice(bi * bt, (bi + 1) * bt)
                yb = yst[bi % 2]
                yt_ = "yst%d" % (bi % 2)
                for half in range(2):
                    b = PSget()
                    for kk in range(4):
                        k = half * 4 + kk
                        tr(ps[:bt, b, kk * 128:(kk + 1) * 128], resT[:, k, cols], id32[:, :], [("resT", k), "id32"], pst(b))
                    cp("act", yb[:bt, half * 512:(half + 1) * 512], ps[:bt, b, :], pst(b),
                       [yt_])
                dst = ys[blk["seq"], :, :] if G["sample"] else yp[blk["seq"], blk["row0"]:blk["row0"] + bt, :]
                S.dma("sp", yt_, dst, yb[:bt, :], reads=[yt_])
            S.barrier()

    def emit_group(G):
        if stop < 1:
            return
        load_x(G)
        for l in range(L):
            if G["sample"]:
                for si, blk in enumerate(G["blocks"]):
                    S.dma("sp", "h_x", histx[:, l, :, si, :], sconv[l, blk["seq"], :, :, :], writes=[("histx", l)])
                    S.dma("sp", "h_s", hists[:, l, :, si, :], ssc[l, blk["seq"], :, :, :], writes=[("hists", l)])
            elif G["blocks"][0]["first"]:
                S.op("dve", lambda e: e.memset(histx[:, l, :, :, :], 0.0), writes=[("histx", l)])
                S.op("dve", lambda e: e.memset(hists[:, l, :, :, :], 0.0), writes=[("hists", l)])
            if stop >= 2:
                for _rep in range(int(os.environ.get("SSDREP", "1"))):
                    phase_ssd(G, l)
            if stop >= 3:
                phase_attn(G, l)
            if stop >= 4:
                phase_sc(G, l)
            if stop >= 5:
                layer_norm(G, l, P_L1G, P_L1B)
            if stop >= 6:
                phase_moe(G, l)
            if stop >= 7:
                layer_norm(G, l, P_L2G, P_L2B)
        if stop >= 8:
            store_y(G)

    groups = []
    if do_sample:
        groups.append(dict(sample=True, nt=32, bt=16, nb=2, nseg=2, segt=16,
                           blocks=[dict(seq=b, first=True, last=True, chunks=[(0, 16)], row0=0) for b in range(2)]))
    for seq in range(2):
        for gi in range(n_pgroups):
            groups.append(dict(sample=False, nt=512, bt=128, nb=4, nseg=1, segt=512,
                               blocks=[dict(seq=seq, first=(gi == 0 and bi == 0),
                                            last=(gi == n_pgroups - 1 and bi == 3),
                                            chunks=[(0, 64), (64, 64)], row0=gi * 512 + bi * 128)
                                       for bi in range(4)]))
    for G in groups:
        emit_group(G)
    S.barrier()
    top.close()
    return nc, S


def prep_shared(inp):
    f = lambda a: np.ascontiguousarray(np.asarray(a, dtype=np.float32))
    w_in = np.asarray(inp["w_in"], np.float32)
    o = {}
    z, xbc, dt, q, k, v, scb, scc, sch, gssd, gatt, gsc = np.split(
        w_in, np.cumsum([2048, 4096, 32, 1024, 256, 256, 1024, 1024, 1024, 1024, 1024])[:], axis=-1)
    kdup = np.concatenate([np.concatenate([k[:, :, g * 64:(g + 1) * 64]] * 2, -1) for g in range(4)], -1)
    wr_ = np.concatenate([z, xbc, gssd, q, kdup, gatt, scc, sch, scb, gsc, k, v], -1)
    assert wr_.shape[-1] == DINR
    pk = lambda w, kc: f(w.reshape(w.shape[:-2] + (kc, 128, w.shape[-1])).swapaxes(-3, -2))
    o["win"] = pk(wr_, 8)
    o["wdt"] = pk(dt, 8)
    o["wso"] = pk(np.asarray(inp["w_ssd_out"], np.float32), 16)
    o["wao"] = pk(np.asarray(inp["w_attn_out"], np.float32), 8)
    o["wsc"] = pk(np.asarray(inp["w_sc_out"], np.float32), 8)
    o["wo"] = pk(np.asarray(inp["w_o"], np.float32), 8)
    wu = np.asarray(inp["w_up"], np.float32)
    gch = wu[..., :1024].reshape(2, 32, 1024, 4, 2, 128)
    uch = wu[..., 1024:].reshape(2, 32, 1024, 4, 2, 128)
    wu_r = np.concatenate([gch, uch], axis=4).reshape(2, 32, 1024, 2048)
    o["wup"] = pk(wu_r, 8)
    del wu_r, gch, uch
    o["wdn"] = pk(np.asarray(inp["w_down"], np.float32), 8)
    o["wr"] = pk(np.asarray(inp["router_w"], np.float32), 8)
    ppt = np.zeros((128, 2, NPP), np.float32)
    bct = np.zeros((128, 2, NBC), np.float32)
    for l in range(2):
        cw = np.asarray(inp["ssd_conv_w"][l], np.float32)
        ppt[:, l, P_CW:P_CW + 128] = cw.reshape(4, 32, 128).transpose(2, 1, 0).reshape(128, 128)
        ppt[:, l, P_CB:P_CB + 32] = np.asarray(inp["ssd_conv_b"][l], np.float32).reshape(32, 128).T
        ppt[:, l, P_NW:P_NW + 16] = np.asarray(inp["ssd_norm_w"][l], np.float32).reshape(16, 128).T
        sw = np.asarray(inp["sc_conv_w"][l], np.float32)
        ppt[:, l, P_SCW:P_SCW + 24] = sw.reshape(3, 8, 128).transpose(2, 1, 0).reshape(128, 24)
        for nm, col in (("ln1_g", P_L1G), ("ln1_b", P_L1B), ("ln2_g", P_L2G), ("ln2_b", P_L2B)):
            ppt[:, l, col:col + 8] = np.asarray(inp[nm][l], np.float32).reshape(8, 128).T
        bu = np.asarray(inp["b_up"][l], np.float32)
        ppt[:, l, P_BUP:P_BUP + 512] = bu.reshape(32, 16, 128).transpose(2, 0, 1).reshape(128, 512)
        bct[:, l, B_DTB:B_DTB + 32] = np.asarray(inp["ssd_dt_bias"][l], np.float32)[None, :]
        bct[:, l, B_ALOG:B_ALOG + 32] = np.asarray(inp["ssd_a_log"][l], np.float32)[None, :]
        bct[:, l, B_DD:B_DD + 32] = np.asarray(inp["ssd_d"][l], np.float32)[None, :]
        bct[:, l, B_SINK:B_SINK + 16] = np.asarray(inp["attn_sinks"][l], np.float32)[None, :]
        bct[:, l, B_RB:B_RB + 32] = np.asarray(inp["router_b"][l], np.float32)[None, :]
    o["pp"] = ppt
    o["bc"] = bct
    o["bdn"] = f(np.asarray(inp["b_down"], np.float32).transpose(1, 0, 2))
    o["tab"] = f(inp["rel_bias"])
    o.update(make_consts())
    return o


def prep_core(inp, i):
    f = lambda a: np.ascontiguousarray(np.asarray(a, dtype=np.float32))
    sl = slice(2 * i, 2 * i + 2)
    o = {}
    o["xp"] = f(inp["x_prompt"][sl])
    o["xs"] = f(inp["x_sample"][sl])
    ck = np.asarray(inp["cache_attn_k"], np.float32)[:, sl]
    o["ckd"] = f(np.concatenate([ck, ck], -1))
    o["cv"] = f(np.asarray(inp["cache_attn_v"], np.float32)[:, sl].reshape(2, 2, 128, 256))
    o["sst"] = f(np.asarray(inp["state_ssd"], np.float32)[:, sl].reshape(2, 2, 2048, 128))
    sc = np.asarray(inp["state_ssd_conv"], np.float32)[:, sl]
    o["sconv"] = f(sc.reshape(2, 2, 3, 32, 128).transpose(0, 1, 4, 3, 2))
    ss = np.asarray(inp["state_short_conv"], np.float32)[:, sl]
    o["ssc"] = f(ss.reshape(2, 2, 2, 8, 128).transpose(0, 1, 4, 3, 2))
    return o


def assemble(results):
    cat = lambda key, ax: np.concatenate([r[key] for r in results], axis=ax)
    y_p = cat("yp", 0)
    y_s = cat("ys", 0)
    k_p = cat("kp", 1).reshape(2, 16, 128, 4, 64)
    v_p = cat("vp", 1).reshape(2, 16, 128, 4, 64)
    st_p = cat("stp", 1).reshape(2, 16, 32, 64, 128)
    cv_p = cat("cvp", 1).transpose(0, 1, 4, 3, 2).reshape(2, 16, 3, 4096)
    sc_p = cat("scp", 1).transpose(0, 1, 4, 3, 2).reshape(2, 16, 2, 1024)
    k_s = cat("ks", 1).reshape(2, 16, 16, 4, 64)
    v_s = cat("vs", 1).reshape(2, 16, 16, 4, 64)
    st_s = cat("sts", 1).reshape(2, 16, 32, 64, 128)
    cv_s = cat("cvs", 1).transpose(0, 1, 4, 3, 2).reshape(2, 16, 3, 4096)
    sc_s = cat("scs", 1).transpose(0, 1, 4, 3, 2).reshape(2, 16, 2, 1024)
    return tuple(np.ascontiguousarray(a, dtype=np.float32) for a in
                 (y_p, y_s, k_p, v_p, st_p, cv_p, sc_p, k_s, v_s, st_s, cv_s, sc_s))


def kernel(**inputs):
    shared = prep_shared(inputs)
    in_maps = []
    for i in range(N_CORES):
        m = dict(shared)
        m.update(prep_core(inputs, i))
        in_maps.append(m)
    nc, _ = build_program()
    res = run_bass_kernel_spmd(nc, in_maps, core_ids=list(range(N_CORES)))
    return assemble(res.results)
```

```python
import math
import os
from contextlib import ExitStack
import numpy as np
import concourse.bass as bass
import concourse.mybir as mybir
from concourse.bass_utils import run_bass_kernel_spmd

F32 = mybir.dt.float32
BF16 = mybir.dt.bfloat16
AF = mybir.ActivationFunctionType
ALU = mybir.AluOpType
AX = mybir.AxisListType

N_CORES = 8
D = 1024
SEQ = 4096
DEC = 16
ALPHA = (2.0 * 2) ** 0.25
LN_EPS = 1e-5
NEG = -30000.0

C_Z, C_XBC, C_GSSD, C_Q, C_KD, C_GATT, C_SCC, C_SCH, C_SCB, C_GSC, C_KV, DINR = (
    0, 2048, 6144, 7168, 8192, 8704, 9728, 10752, 11776, 12800, 13824, 14336)
P_CW, P_CB, P_NW, P_SCW, P_L1G, P_L1B, P_L2G, P_L2B, P_BUP, NPP = 0, 128, 160, 176, 200, 208, 216, 224, 232, 744
B_DTB, B_ALOG, B_DD, B_SINK, B_RB, NBC = 0, 32, 64, 96, 112, 144

NSLOT = 4
ASTOP_G = int(os.environ.get('ASTOP', '99'))


class Sched:
    def __init__(self, nc):
        self.nc = nc
        self.eng = {"pe": nc.tensor, "dve": nc.vector, "act": nc.scalar, "pool": nc.gpsimd, "sp": nc.sync}
        self.sem = {}
        self.cnt = {}
        self._ctx = []
        for k in self.eng:
            cm = nc.semaphore("s_" + k)
            s = cm.__enter__()
            self._ctx.append(cm)
            self.sem[k] = s
            self.cnt[k] = 0
        self.waited = {k: {} for k in self.eng}
        self.semobj = {k: self.sem[k] for k in self.eng}
        self.tok_w = {}
        self.tok_r = {}
        self.dma_sems = {}
        self.ninst = 0

    def dma_sem(self, name):
        if name not in self.dma_sems:
            cm = self.nc.semaphore("d_" + name)
            s = cm.__enter__()
            self._ctx.append(cm)
            self.dma_sems[name] = [s, 0]
            self.semobj["d_" + name] = s
        return self.dma_sems[name]

    def _wait(self, e, semkey, val):
        if self.waited[e].get(semkey, 0) >= val:
            return
        self.eng[e].wait_ge(self.semobj[semkey], val)
        self.waited[e][semkey] = val

    def _deps(self, e, reads, writes):
        for t in reads:
            for sk, v in self.tok_w.get(t, {}).items():
                if sk == "pe" and e == "pe":
                    continue
                self._wait(e, sk, v)
        for t in writes:
            for sk, v in self.tok_w.get(t, {}).items():
                if sk == e:
                    continue
                self._wait(e, sk, v)
            for sk, v in self.tok_r.get(t, {}).items():
                if sk == e:
                    continue
                self._wait(e, sk, v)

    def _mark(self, semkey, val, reads, writes):
        for t in reads:
            self.tok_r.setdefault(t, {})[semkey] = val
        for t in writes:
            self.tok_w[t] = {semkey: val}
            self.tok_r[t] = {}

    def op(self, e, fn, reads=(), writes=()):
        self._deps(e, reads, writes)
        ins = fn(self.eng[e])
        self.cnt[e] += 1
        ins.then_inc(self.sem[e], 1)
        self._mark(e, self.cnt[e], reads, writes)
        self.ninst += 1

    def dma(self, q, semname, out, in_, reads=(), writes=(), **kw):
        self._deps(q, reads, writes)
        s = self.dma_sem(semname)
        ins = self.eng[q].dma_start(out=out, in_=in_, **kw)
        s[1] += 16
        ins.then_inc(s[0], 16)
        self._mark("d_" + semname, s[1], reads, writes)
        self.ninst += 1

    def barrier(self):
        for e in self.eng:
            for k in self.eng:
                if k != e and self.cnt[k] > 0:
                    self._wait(e, k, self.cnt[k])
            for name, (s, c) in self.dma_sems.items():
                if c > 0:
                    self._wait(e, "d_" + name, c)


def t5_bucket_np(rel):
    half, max_exact = 16, 8
    ret = np.where(rel > 0, half, 0)
    n = np.abs(rel)
    nf = np.maximum(n, 1).astype(np.float32)
    large = max_exact + (np.log(nf / np.float32(max_exact)) / np.float32(math.log(128 / max_exact))
                         * np.float32(half - max_exact)).astype(np.int32)
    large = np.minimum(large, half - 1)
    return ret + np.where(n < max_exact, n, large)


def make_consts():
    t = np.arange(128)
    same = (t[:, None] // 64) == (t[None, :] // 64)
    U = (same & (t[:, None] <= t[None, :])).astype(np.float32)
    Ls = (same & (t[:, None] > t[None, :])).astype(np.float32)
    SC = same.astype(np.float32)
    CI0 = np.repeat((t < 64).astype(np.float32)[:, None], 128, 1)
    CI1 = np.repeat((t >= 64).astype(np.float32)[:, None], 128, 1)
    msk = np.stack([U, Ls, SC, CI0, CI1], 1)
    j = np.arange(256)
    qa = t[:, None] < 64
    valid = np.where(qa, j[None, :] < 192, j[None, :] >= 64)
    bmmask = np.where(valid, 0.0, NEG).astype(np.float32)
    rel = np.arange(383) - 255
    bk = t5_bucket_np(rel)
    ohd = np.zeros((32, 383), np.float32)
    ohd[bk, np.arange(383)] = 1.0
    return dict(msk=np.ascontiguousarray(msk), id32=np.eye(128, dtype=np.float32),
                ones32=np.ones((128, 128), np.float32), bmmask=bmmask, ohd=ohd)


def build_program(n_pgroups=8, do_sample=True, n_layers=2, stop=99):
    nc = bass.Bass("TRN2", target_bir_lowering=False)
    L = n_layers

    def din(name, shape):
        return nc.dram_tensor(name, list(shape), F32, kind="ExternalInput").ap()

    def dout(name, shape):
        return nc.dram_tensor(name, list(shape), F32, kind="ExternalOutput").ap()

    xp = din("xp", [2, SEQ, D]); xs_in = din("xs", [2, DEC, D])
    ckd = din("ckd", [2, 2, 128, 4, 128]); cv = din("cv", [2, 2, 128, 256])
    sst = din("sst", [2, 2, 2048, 128]); sconv = din("sconv", [2, 2, 128, 32, 3]); ssc = din("ssc", [2, 2, 128, 8, 2])
    win = din("win", [2, 128, 8, DINR]); wdt_d = din("wdt", [2, 128, 8, 32])
    wso = din("wso", [2, 128, 16, 1024]); wao = din("wao", [2, 128, 8, 1024])
    wsc = din("wsc", [2, 128, 8, 1024]); wo = din("wo", [2, 128, 8, 1024])
    wup = din("wup", [2, 32, 128, 8, 2048]); wdn = din("wdn", [2, 32, 128, 8, 1024])
    wr_d = din("wr", [2, 128, 8, 32])
    pp_d = din("pp", [128, 2, NPP]); bc_d = din("bc", [128, 2, NBC]); bdn_d = din("bdn", [32, 2, 1024])
    msk_d = din("msk", [128, 5, 128]); id32_d = din("id32", [128, 128]); ones_d = din("ones32", [128, 128])
    bmmask_d = din("bmmask", [128, 256]); ohd_d = din("ohd", [32, 383]); tab_d = din("tab", [32, 16])

    yp = dout("yp", [2, SEQ, D]); ys = dout("ys", [2, DEC, D])
    kp = dout("kp", [2, 2, 128, 256]); vp = dout("vp", [2, 2, 128, 256])
    stp = dout("stp", [2, 2, 2048, 128]); cvp = dout("cvp", [2, 2, 128, 32, 3]); scp = dout("scp", [2, 2, 128, 8, 2])
    ks = dout("ks", [2, 2, DEC, 256]); vs = dout("vs", [2, 2, DEC, 256])
    sts = dout("sts", [2, 2, 2048, 128]); cvs = dout("cvs", [2, 2, 128, 32, 3]); scs = dout("scs", [2, 2, 128, 8, 2])
    scr = nc.dram_tensor("scr_tb", [16, 383], F32, kind="Internal").ap()

    def scr_t(name, shape):
        return nc.dram_tensor(name, list(shape), BF16, kind="Internal").ap()

    s_win = scr_t("s_win", [2, 28, 128, 4096]); s_wso = scr_t("s_wso", [2, 4, 128, 4096])
    s_wao = scr_t("s_wao", [2, 2, 128, 4096]); s_wsc = scr_t("s_wsc", [2, 2, 128, 4096])
    s_wo = scr_t("s_wo", [2, 2, 128, 4096])
    s_wup = scr_t("s_wup", [2, 32, 4, 128, 4096]); s_wdn = scr_t("s_wdn", [2, 32, 2, 128, 4096])

    S = Sched(nc)
    top = ExitStack()

    uid = [0]

    def sb(stack, name, shape, dtype):
        uid[0] += 1
        return stack.enter_context(nc.sbuf_tensor("sb%d_%s" % (uid[0], name), list(shape), dtype))

    ps = top.enter_context(nc.psum_tensor("ps", [128, 8, 512], F32))
    psn = [0]

    def PSget(n=1):
        if psn[0] + n > 8:
            psn[0] = 0
        b = psn[0]
        psn[0] = (psn[0] + n) % 8
        return b

    def pst(b, n=1):
        return [("ps", b + i) for i in range(n)]

    def psb(b):
        return ps[:, b, :].bitcast(BF16)

    ring = sb(top, "ring", [128, NSLOT, 4096], BF16)
    resT = sb(top, "resT", [128, 8, 512], F32)
    xT = sb(top, "xT", [128, 8, 512], BF16)
    Sst = sb(top, "Sst", [128, 2, 2048], F32)
    Sbf = sb(top, "Sbf", [128, 2048], BF16)
    BM = sb(top, "BM", [128, 16, 256], F32)
    msk = sb(top, "msk", [128, 5, 128], F32)
    id32 = sb(top, "id32", [128, 128], F32)
    idb = sb(top, "idb", [128, 128], BF16)
    ones32 = sb(top, "ones32", [128, 128], F32)
    pp = sb(top, "pp", [128, 2, NPP], F32)
    bc = sb(top, "bc", [128, 2, NBC], F32)
    bdn = sb(top, "bdn", [32, 2, 1024], F32)
    wdt = sb(top, "wdt", [128, 2, 8, 32], F32)
    wr = sb(top, "wr", [128, 2, 8, 32], F32)
    Abc = sb(top, "Abc", [128, 2, 32], F32)
    bu7 = sb(top, "bu7", [128, 2, 32, 8], F32)
    histx = sb(top, "histx", [128, 2, 32, 2, 3], F32)
    hists = sb(top, "hists", [128, 2, 8, 2, 2], F32)
    kprev = sb(top, "kprev", [128, 2, 4, 128], BF16)
    vprev = sb(top, "vprev", [128, 2, 256], BF16)

    rslot = [0]

    def wload(key, kdim=8):
        s = rslot[0]
        rslot[0] = (s + 1) % NSLOT
        if not cv_done[0]:
            cnts = cv_counts[cv_tile[key]]
            for j in range(8):
                if cnts[j] > 0:
                    S._wait("sp", "d_cv%d" % j, cnts[j])
        tile_ap = tile_of(key)
        S.dma("sp", "ring%d" % s, ring[:, s, :], tile_ap, writes=[("ring", s)])
        return s

    def tile_of(key):
        nm = key[0]
        if nm == "win":
            return s_win[key[1], key[2]]
        if nm == "wso":
            return s_wso[key[1], key[2]]
        if nm == "wao":
            return s_wao[key[1], key[2]]
        if nm == "wsc":
            return s_wsc[key[1], key[2]]
        if nm == "wo":
            return s_wo[key[1], key[2]]
        if nm == "wup":
            return s_wup[key[1], key[2], key[3]]
        return s_wdn[key[1], key[2], key[3]]

    cvn = [0]
    cv_counts = []
    cv_tile = {}
    cv_done = [False]

    def conv(key, dst_tile, src, kdim):
        i = cvn[0]
        cvn[0] += 1
        S.dma("pool", "cv%d" % (i % 8), dst_tile.rearrange("p (k c) -> p k c", k=kdim), src)
        prev = list(cv_counts[-1]) if cv_counts else [0] * 8
        prev[i % 8] += 16
        cv_counts.append(prev)
        cv_tile[key] = i

    def ring8(s):
        return ring[:, s, :].rearrange("p (k c) -> p k c", k=8)

    def ring16(s):
        return ring[:, s, :].rearrange("p (k c) -> p k c", k=16)

    def mm(out, lhsT, rhs, start, stop, reads, writes):
        S.op("pe", lambda e: e.matmul(out, lhsT=lhsT, rhs=rhs, start=start, stop=stop), reads=reads, writes=writes)

    def tr(out, in_, ident, reads, writes):
        S.op("pe", lambda e: e.transpose(out=out, in_=in_, identity=ident), reads=reads, writes=writes)

    def act(out, in_, func, reads, writes, **kw):
        S.op("act", lambda e: e.activation(out=out, in_=in_, func=func, **kw), reads=reads, writes=writes)

    def tt(out, in0, in1, op, reads, writes, eng="dve"):
        S.op(eng, lambda e: e.tensor_tensor(out=out, in0=in0, in1=in1, op=op), reads=reads, writes=writes)

    def ts(out, in0, s1, s2, op0, op1, reads, writes, eng="dve"):
        if op1 is None:
            S.op(eng, lambda e: e.tensor_scalar(out=out, in0=in0, scalar1=s1, scalar2=None, op0=op0),
                 reads=reads, writes=writes)
        else:
            S.op(eng, lambda e: e.tensor_scalar(out=out, in0=in0, scalar1=s1, scalar2=s2, op0=op0, op1=op1),
                 reads=reads, writes=writes)

    def stt(out, in0, scalar, in1, op0, op1, reads, writes):
        S.op("dve", lambda e: e.scalar_tensor_tensor(out=out, in0=in0, scalar=scalar, in1=in1, op0=op0, op1=op1),
             reads=reads, writes=writes)

    def cp(eng, out, in_, reads, writes):
        if eng == "act":
            act(out, in_, AF.Copy, reads, writes)
        else:
            S.op(eng, lambda e: e.tensor_copy(out=out, in_=in_), reads=reads, writes=writes)

    def bcast(ap, axis, shape):
        return ap.unsqueeze(axis).broadcast_to(list(shape))

    S.dma("sp", "c_msk", msk[:], msk_d[:, :, :], writes=["msk"])
    S.dma("sp", "c_id", id32[:], id32_d[:, :], writes=["id32"])
    S.dma("sp", "c_ones", ones32[:], ones_d[:, :], writes=["ones32"])
    S.dma("sp", "c_pp", pp[:], pp_d[:, :, :], writes=["pp"])
    S.dma("sp", "c_bc", bc[:], bc_d[:, :, :], writes=["bc"])
    S.dma("sp", "c_bdn", bdn[:], bdn_d[:, :, :], writes=["bdn"])
    S.dma("sp", "c_wdt", wdt[:], wdt_d.rearrange("l p k c -> p l k c"), writes=["wdt"])
    S.dma("sp", "c_wr", wr[:], wr_d.rearrange("l p k c -> p l k c"), writes=["wr"])
    cp("dve", idb[:], id32[:], ["id32"], ["idb"])
    act(Abc[:], bc[:, :, B_ALOG:B_ALOG + 32], AF.Exp, ["bc"], ["Abc"])
    ts(Abc[:], Abc[:], -1.0, None, ALU.mult, None, ["Abc"], ["Abc"])
    for l in range(2):
        ts(bu7[:, l, :, :], pp[:, l, P_BUP:P_BUP + 512].rearrange("p (e f) -> p e f", f=16)[:, :, 8:16],
           7.0, None, ALU.add, None, ["pp"], ["bu7"])
    with ExitStack() as st:
        ohd = sb(st, "ohd", [32, 383], F32)
        tab = sb(st, "tab", [32, 16], F32)
        tbs = sb(st, "tbs", [16, 383], F32)
        bmm = sb(st, "bmm", [128, 256], F32)
        S.dma("sp", "c_ohd", ohd[:], ohd_d[:, :], writes=["ohd"])
        S.dma("sp", "c_tab", tab[:], tab_d[:, :], writes=["tab"])
        S.dma("sp", "c_bmm", bmm[:], bmmask_d[:, :], writes=["bmm"])
        b = PSget()
        mm(ps[:16, b, 0:383], tab[:, :], ohd[:, :], True, True, ["tab", "ohd"], pst(b))
        cp("dve", tbs[:], ps[:16, b, 0:383], pst(b), ["tbs"])
        S.dma("sp", "c_scr", scr[:, :], tbs[:], reads=["tbs"], writes=["scr"])
        for q in range(128):
            S.dma("sp", "c_BM", BM[q:q + 1, :, :], scr[:, 127 - q:127 - q + 256].unsqueeze(0),
                  reads=["scr"], writes=["BM"])
        tt(BM[:], BM[:], bcast(bmm[:], 1, [128, 16, 256]), ALU.add, ["BM", "bmm"], ["BM"])
        S.barrier()

    for l in range(L):
        for t in range(28):
            conv(("win", l, t), s_win[l, t], win[l, :, :, t * 512:(t + 1) * 512], 8)
        for t in range(4):
            conv(("wso", l, t), s_wso[l, t], wso[l, :, :, t * 256:(t + 1) * 256], 16)
        for t in range(2):
            conv(("wao", l, t), s_wao[l, t], wao[l, :, :, t * 512:(t + 1) * 512], 8)
            conv(("wsc", l, t), s_wsc[l, t], wsc[l, :, :, t * 512:(t + 1) * 512], 8)
            conv(("wo", l, t), s_wo[l, t], wo[l, :, :, t * 512:(t + 1) * 512], 8)
        for e_ in range(32):
            for j in range(4):
                conv(("wup", l, e_, j), s_wup[l, e_, j], wup[l, e_, :, :, j * 512:(j + 1) * 512], 8)
            for j in range(2):
                conv(("wdn", l, e_, j), s_wdn[l, e_, j], wdn[l, e_, :, :, j * 512:(j + 1) * 512], 8)

    def load_x(G):
        nt, bt, nb = G["nt"], G["bt"], G["nb"]
        with ExitStack() as st:
            xin = [sb(st, "xin%d" % i, [128, 1024], F32) for i in range(2)]
            for bi, blk in enumerate(G["blocks"]):
                xi = xin[bi % 2]
                src = xs_in[blk["seq"], :, :] if G["sample"] else xp[blk["seq"], blk["row0"]:blk["row0"] + bt, :]
                S.dma("sp", "xin%d" % (bi % 2), xi[:bt, :], src, writes=["xin%d" % (bi % 2)])
                for half in range(2):
                    b = PSget()
                    for kk in range(4):
                        k = half * 4 + kk
                        tr(ps[:, b, kk * 128:kk * 128 + bt], xi[:bt, k * 128:(k + 1) * 128], id32[:bt, :bt],
                           ["xin%d" % (bi % 2), "id32"], pst(b))
                    src_ps = ps[:, b, :].rearrange("p (k t) -> p k t", k=4)[:, :, :bt]
                    cp("act", resT[:, half * 4:half * 4 + 4, bi * bt:(bi + 1) * bt], src_ps, pst(b),
                       [("resT", half * 4 + kk) for kk in range(4)])
                    cp("act", xT[:, half * 4:half * 4 + 4, bi * bt:(bi + 1) * bt], src_ps, pst(b), ["xT"])
            S.barrier()

    def proj_fm(slot, c0, nt, src=None, srctok="xT", kdim=8):
        b = PSget()
        rv = ring8(slot) if kdim == 8 else ring16(slot)
        xsrc = xT if src is None else src
        for k in range(kdim):
            mm(ps[:, b, :nt], rv[:, k, c0:c0 + 128], xsrc[:, k, :nt], k == 0, k == kdim - 1,
               [("ring", slot), srctok], pst(b))
        return b

    def wo_accumulate(G, l, gated, first):
        nt = G["nt"]
        for t in range(2):
            slot = wload(("wo", l, t))
            for cc in range(4):
                c = 4 * t + cc
                b = proj_fm(slot, cc * 128, nt, src=gated, srctok="gated")
                if first:
                    stt(resT[:, c, :nt], resT[:, c, :nt], ALPHA, ps[:, b, :nt], ALU.mult, ALU.add,
                        [("resT", c)] + pst(b), [("resT", c)])
                else:
                    tt(resT[:, c, :nt], ps[:, b, :nt], resT[:, c, :nt], ALU.add, [("resT", c)] + pst(b),
                       [("resT", c)])

    def gate_and_out(G, l, st, gcol, wsrc_fn, ntile, kdim, ysrc, ysrctok, first):
        nt = G["nt"]
        gate = sb(st, "gate", [128, 8, nt], BF16)
        gated = sb(st, "gated", [128, 8, nt], BF16)
        for t in range(2):
            slot = wload(("win", l, gcol // 512 + t))
            for cc in range(4):
                c = 4 * t + cc
                b = proj_fm(slot, cc * 128, nt)
                act(gate[:, c, :nt], ps[:, b, :nt], AF.Sigmoid, pst(b), ["gate"])
        per = 8 // ntile
        for t in range(ntile):
            slot = wload(wsrc_fn(t))
            for cc in range(per):
                c = per * t + cc
                b = proj_fm(slot, cc * 128, nt, src=ysrc, srctok=ysrctok, kdim=kdim)
                tt(gated[:, c, :nt], ps[:, b, :nt], gate[:, c, :nt], ALU.mult, pst(b) + ["gate"], ["gated"])
        wo_accumulate(G, l, gated, first)

    def phase_ssd(G, l):
        nt, bt, nb, nseg, segt = G["nt"], G["bt"], G["nb"], G["nseg"], G["segt"]
        sample = G["sample"]
        with ExitStack() as st:
            zs = sb(st, "zs", [128, nb, 2048], BF16)
            xc = sb(st, "xc", [128, 32, nt], BF16)
            ynT = sb(st, "ynT", [128, 16, nt], BF16)
            dtt = sb(st, "dtt", [128, nb, 32], F32)
            adt = sb(st, "adt", [128, nb, 32], F32)
            for zt in range(4):
                slot = wload(("win", l, C_Z // 512 + zt))
                for bi in range(nb):
                    b = PSget()
                    for k in range(8):
                        mm(ps[:bt, b, :], xT[:, k, bi * bt:(bi + 1) * bt], ring8(slot)[:, k, :], k == 0, k == 7,
                           ["xT", ("ring", slot)], pst(b))
                    act(zs[:bt, bi, zt * 512:(zt + 1) * 512], ps[:bt, b, :], AF.Silu, pst(b), [("zs", bi)])
            with ExitStack() as st2:
                raw = [sb(st2, "raw%d" % i, [128, nseg, 3 + segt], F32) for i in range(2)]
                cacc = [sb(st2, "cacc%d" % i, [128, nseg, segt], F32) for i in range(2)]
                for t in range(8):
                    slot = wload(("win", l, C_XBC // 512 + t))
                    for cc in range(4):
                        c = 4 * t + cc
                        r, a = raw[c % 2], cacc[c % 2]
                        rt, at = "raw%d" % (c % 2), "cacc%d" % (c % 2)
                        b = proj_fm(slot, cc * 128, nt)
                        cp("act", r[:, :, 3:3 + segt], ps[:, b, :nt].rearrange("p (s t) -> p s t", s=nseg),
                           pst(b), [rt])
                        cp("dve", r[:, :, 0:3], histx[:, l, c, :nseg, :], [("histx", l)], [rt])
                        cp("dve", histx[:, l, c, :nseg, :], r[:, :, segt:segt + 3], [rt], [("histx", l)])
                        ts(a[:], r[:, :, 0:segt], pp[:, l, P_CW + 4 * c:P_CW + 4 * c + 1], None, ALU.mult, None,
                           [rt, "pp"], [at])
                        for j in range(1, 4):
                            stt(a[:], r[:, :, j:j + segt], pp[:, l, P_CW + 4 * c + j:P_CW + 4 * c + j + 1], a[:],
                                ALU.mult, ALU.add, [rt, at, "pp"], [at])
                        act(xc[:, c, :nt].rearrange("p (s t) -> p s t", s=nseg), a[:], AF.Silu, [at, "pp"],
                            [("xc", c)], bias=pp[:, l, P_CB + c:P_CB + c + 1], scale=1.0)
                for si in range(nseg):
                    blk = G["blocks"][si if sample else nb - 1]
                    if blk["last"]:
                        dst = (cvs if sample else cvp)[l, blk["seq"], :, :, :]
                        S.dma("sp", "o_cv", dst, histx[:, l, :, si, :], reads=[("histx", l)])
                tmpa = sb(st2, "tmpa", [128, 32], F32)
                tmpb = sb(st2, "tmpb", [128, 32], F32)
                for bi in range(nb):
                    b = PSget()
                    for k in range(8):
                        mm(ps[:bt, b, 0:32], resT[:, k, bi * bt:(bi + 1) * bt], wdt[:, l, k, :], k == 0, k == 7,
                           [("resT", k), "wdt"], pst(b))
                    tt(tmpa[:bt, :], ps[:bt, b, 0:32], bc[:bt, l, B_DTB:B_DTB + 32], ALU.add, pst(b) + ["bc"],
                       ["tmpa"])
                    act(tmpb[:bt, :], tmpa[:bt, :], AF.Exp, ["tmpa"], ["tmpb"])
                    ts(tmpa[:bt, :], tmpb[:bt, :], 1.0, None, ALU.add, None, ["tmpb"], ["tmpa"])
                    act(dtt[:bt, bi, :], tmpa[:bt, :], AF.Ln, ["tmpa"], ["dtt"])
                    tt(adt[:bt, bi, :], dtt[:bt, bi, :], Abc[:bt, l, :], ALU.mult, ["dtt", "Abc"], ["adt"])
                S.barrier()
            with ExitStack() as st2:
                acs = sb(st2, "acs", [128, 32], F32)
                ea = sb(st2, "ea", [128, 32], F32)
                d2e = sb(st2, "d2e", [128, 32], F32)
                dtd = sb(st2, "dtd", [128, 2, 32], F32)
                tmpd = sb(st2, "tmpd", [128, 32], F32)
                cdec = sb(st2, "cdec", [128, 2, 32], F32)
                Btok = sb(st2, "Btok", [128, 8, 128], BF16)
                CTm = sb(st2, "CTm", [128, 2, 8, 128], BF16)
                stg = sb(st2, "stg", [128, 4, 128], F32)
                xs_t = sb(st2, "xs_t", [128, 256], BF16)
                xsD = sb(st2, "xsD", [128, 256], BF16)
                xdt = sb(st2, "xdt", [128, 256], BF16)
                xdtd = sb(st2, "xdtd", [128, 2, 256], BF16)
                R = sb(st2, "R", [128, 4, bt], F32)
                E = sb(st2, "E", [128, 4, bt], F32)
                cbm = sb(st2, "cbm", [128, bt], F32)
                M = sb(st2, "M", [128, 4, bt], BF16)
                yo = sb(st2, "yo", [128, 256], F32)
                yg = sb(st2, "yg", [128, 256], F32)
                junk = sb(st2, "junk", [128, 256], F32)
                yn = sb(st2, "yn", [128, 256], BF16)
                sm = sb(st2, "sm", [128, 4], F32)
                S.op("dve", lambda e: e.memset(CTm[:], 0.0), writes=["CTm"])
                for bi, blk in enumerate(G["blocks"]):
                    cols = slice(bi * bt, (bi + 1) * bt)
                    chunks = blk["chunks"]
                    if blk["first"]:
                        if sample:
                            for c4 in range(4):
                                S.dma("sp", "stg", stg[:],
                                      sst[l, blk["seq"], c4 * 512:(c4 + 1) * 512, :].rearrange("(c p) n -> p c n", p=128),
                                      writes=["stg"])
                                b = PSget()
                                for i in range(4):
                                    tr(ps[:, b, i * 128:(i + 1) * 128], stg[:, i, :], id32[:, :], ["stg", "id32"],
                                       pst(b))
                                cp("act", Sst[:, l, c4 * 512:(c4 + 1) * 512], ps[:, b, :], pst(b),
                                   [("S", l, c4 * 2), ("S", l, c4 * 2 + 1)])
                        else:
                            S.op("dve", lambda e: e.memset(Sst[:, l, :], 0.0), writes=[("S", l, g) for g in range(8)])
                    if bi == 0 or sample:
                        cp("act", Sbf[:], Sst[:, l, :], [("S", l, g) for g in range(8)],
                           [("Sbf", g) for g in range(8)])
                    b = PSget()
                    mm(ps[:bt, b, 0:32], msk[:bt, 0, :bt], adt[:bt, bi, :], True, True, ["msk", "adt"], pst(b))
                    mm(ps[:bt, b, 32:64], msk[:bt, 2, :bt], adt[:bt, bi, :], True, True, ["msk", "adt"], pst(b))
                    cp("act", acs[:bt, :], ps[:bt, b, 0:32], pst(b), ["acs"])
                    act(ea[:bt, :], ps[:bt, b, 0:32], AF.Exp, pst(b), ["ea"])
                    tt(tmpd[:bt, :], ps[:bt, b, 32:64], acs[:bt, :], ALU.subtract, pst(b) + ["acs"], ["tmpd"])
                    act(d2e[:bt, :], tmpd[:bt, :], AF.Exp, ["tmpd"], ["d2e"])
                    tt(dtd[:bt, 0, :], dtt[:bt, bi, :], d2e[:bt, :], ALU.mult, ["dtt", "d2e"], ["dtd"])
                    if len(chunks) == 2:
                        ts(dtd[:bt, 1, :], dtd[:bt, 0, :], msk[:bt, 4, 0:1], None, ALU.mult, None, ["dtd", "msk"],
                           ["dtd"])
                        ts(dtd[:bt, 0, :], dtd[:bt, 0, :], msk[:bt, 3, 0:1], None, ALU.mult, None, ["dtd", "msk"],
                           ["dtd"])
                    for j in range(len(chunks)):
                        b2 = PSget()
                        mm(ps[:, b2, 0:32], msk[:bt, 3 + j, :], adt[:bt, bi, :], True, True, ["msk", "adt"], pst(b2))
                        act(cdec[:, j, :], ps[:, b2, 0:32], AF.Exp, pst(b2), ["cdec"])
                    b = PSget()
                    for g in range(8):
                        tr(psb(b)[:bt, g * 128:(g + 1) * 128], xc[:, 16 + g, cols], idb[:, :],
                           [("xc", 16 + g), "idb"], pst(b))
                    cp("act", Btok[:bt, :, :], psb(b)[:bt, :].rearrange("p (g n) -> p g n", g=8), pst(b), ["Btok"])
                    if len(chunks) == 2:
                        for j, (p0, ln) in enumerate(chunks):
                            cp("dve", CTm[:, j, :, p0:p0 + ln], xc[:, 24:32, bi * bt + p0:bi * bt + p0 + ln],
                               [("xc", 24 + g) for g in range(8)], ["CTm"])
                    for g in range(8):
                        h4 = slice(4 * g, 4 * g + 4)
                        b = PSget()
                        for i in range(2):
                            tr(psb(b)[:bt, i * 128:(i + 1) * 128], xc[:, 2 * g + i, cols], idb[:, :],
                               [("xc", 2 * g + i), "idb"], pst(b))
                        cp("act", xs_t[:bt, :], psb(b)[:bt, 0:256], pst(b), ["xs_t"])
                        xs3 = xs_t[:bt, :].rearrange("p (h d) -> p h d", h=4)
                        tt(xsD[:bt, :].rearrange("p (h d) -> p h d", h=4), xs3,
                           bcast(bc[:bt, l, B_DD + 4 * g:B_DD + 4 * g + 4], 2, [bt, 4, 64]), ALU.mult,
                           ["xs_t", "bc"], ["xsD"])
                        tt(xdt[:bt, :].rearrange("p (h d) -> p h d", h=4), xs3,
                           bcast(dtt[:bt, bi, h4], 2, [bt, 4, 64]), ALU.mult, ["xs_t", "dtt"], ["xdt"])
                        for j in range(len(chunks)):
                            tt(xdtd[:bt, j, :].rearrange("p (h d) -> p h d", h=4), xs3,
                               bcast(dtd[:bt, j, h4], 2, [bt, 4, 64]), ALU.mult, ["xs_t", "dtd"], ["xdtd"])
                        tt(R[:bt, :, :], bcast(msk[:bt, 0, :bt], 1, [bt, 4, bt]),
                           bcast(adt[:bt, bi, h4], 2, [bt, 4, bt]), ALU.mult, ["msk", "adt"], ["R"])
                        b = PSget()
                        mm(ps[:bt, b, 0:4 * bt], msk[:bt, 1, :bt], R[:bt, :, :].rearrange("p h l -> p (h l)"),
                           True, True, ["msk", "R"], pst(b))
                        act(E[:bt, :, :].rearrange("p h l -> p (h l)"), ps[:bt, b, 0:4 * bt], AF.Exp, pst(b), ["E"])
                        b = PSget()
                        mm(ps[:bt, b, 0:bt], xc[:, 16 + g, cols], xc[:, 24 + g, cols], True, True,
                           [("xc", 16 + g), ("xc", 24 + g)], pst(b))
                        tt(cbm[:bt, :], ps[:bt, b, 0:bt], msk[:bt, 0, :bt], ALU.mult, pst(b) + ["msk"], ["cbm"])
                        tt(M[:bt, :, :], E[:bt, :, :], bcast(cbm[:bt, :], 1, [bt, 4, bt]), ALU.mult, ["E", "cbm"],
                           ["M"])
                        byo = PSget()
                        for j, (p0, ln) in enumerate(chunks):
                            if len(chunks) == 2:
                                lc, lct = CTm[:, j, g, :bt], "CTm"
                            else:
                                lc, lct = xc[:, 24 + g, cols], ("xc", 24 + g)
                            mm(ps[:bt, byo, 0:256], lc, Sbf[:, g * 256:(g + 1) * 256], j == 0, j == len(chunks) - 1,
                               [lct, ("Sbf", g)], pst(byo))
                            bds = PSget()
                            mm(ps[:, bds, 0:256], Btok[:bt, g, :], xdtd[:bt, j, :], True, True,
                               ["Btok", "xdtd"], pst(bds))
                            Sg = Sst[:, l, g * 256:(g + 1) * 256]
                            tt(Sg.rearrange("p (h d) -> p h d", h=4), Sg.rearrange("p (h d) -> p h d", h=4),
                               bcast(cdec[:, j, h4], 2, [128, 4, 64]), ALU.mult, [("S", l, g), "cdec"], [("S", l, g)])
                            tt(Sg, ps[:, bds, 0:256], Sg, ALU.add, pst(bds) + [("S", l, g)], [("S", l, g)])
                            cp("act", Sbf[:, g * 256:(g + 1) * 256], Sg, [("S", l, g)], [("Sbf", g)])
                        tt(yo[:bt, :].rearrange("p (h d) -> p h d", h=4),
                           ps[:bt, byo, 0:256].rearrange("p (h d) -> p h d", h=4),
                           bcast(ea[:bt, h4], 2, [bt, 4, 64]), ALU.mult, pst(byo) + ["ea"], ["yo"])
                        by = PSget()
                        for hh in range(4):
                            mm(ps[:bt, by, hh * 64:(hh + 1) * 64], M[:bt, hh, :], xdt[:bt, hh * 64:(hh + 1) * 64],
                               True, False, ["M", "xdt"], pst(by))
                            mm(ps[:bt, by, hh * 64:(hh + 1) * 64], idb[:bt, :bt], xsD[:bt, hh * 64:(hh + 1) * 64],
                               False, True, ["idb", "xsD"], pst(by))
                        tt(yg[:bt, :], ps[:bt, by, 0:256], yo[:bt, :], ALU.add, pst(by) + ["yo"], ["yg"])
                        tt(yg[:bt, :], yg[:bt, :], zs[:bt, bi, g * 256:(g + 1) * 256], ALU.mult, ["yg", ("zs", bi)],
                           ["yg"])
                        act(junk[:bt, :], yg[:bt, :], AF.Square, ["yg"], ["junk", "sm"], accum_out=sm[:bt, 0:1])
                        ts(sm[:bt, 1:2], sm[:bt, 0:1], 1.0 / 256, LN_EPS, ALU.mult, ALU.add, ["sm"], ["sm"])
                        act(sm[:bt, 2:3], sm[:bt, 1:2], AF.Sqrt, ["sm"], ["sm"])
                        S.op("dve", lambda e: e.reciprocal(out=sm[:bt, 3:4], in_=sm[:bt, 2:3]), reads=["sm"],
                             writes=["sm"])
                        ts(yn[:bt, :], yg[:bt, :], sm[:bt, 3:4], None, ALU.mult, None, ["yg", "sm"], ["yn"])
                        b = PSget()
                        for i in range(2):
                            tr(psb(b)[:, i * 128:i * 128 + bt], yn[:bt, i * 128:(i + 1) * 128], idb[:bt, :bt],
                               ["yn", "idb"], pst(b))
                        tt(ynT[:, 2 * g:2 * g + 2, cols],
                           psb(b)[:, 0:256].rearrange("p (i t) -> p i t", i=2)[:, :, :bt],
                           bcast(pp[:, l, P_NW + 2 * g:P_NW + 2 * g + 2], 2, [128, 2, bt]), ALU.mult,
                           pst(b) + ["pp"], ["ynT"])
                    if blk["last"]:
                        for c4 in range(4):
                            b = PSget()
                            for i in range(4):
                                c = 4 * c4 + i
                                tr(ps[:, b, i * 128:(i + 1) * 128], Sst[:, l, c * 128:(c + 1) * 128], id32[:, :],
                                   [("S", l, c // 2), "id32"], pst(b))
                            cp("act", stg[:, :, :], ps[:, b, :].rearrange("p (i n) -> p i n", i=4),
                               pst(b), ["stg"])
                            dst = (sts if sample else stp)[l, blk["seq"], c4 * 512:(c4 + 1) * 512, :].rearrange(
                                "(c p) n -> p c n", p=128)
                            S.dma("sp", "stg", dst, stg[:], reads=["stg"])
                S.barrier()
            with ExitStack() as st2:
                gate_and_out(G, l, st2, C_GSSD, lambda t: ("wso", l, t), 4, 16, ynT, "ynT",
                             True)
                S.barrier()

    def phase_attn(G, l):
        nt, bt, nb = G["nt"], G["bt"], G["nb"]
        sample = G["sample"]
        with ExitStack() as st:
            qT = sb(st, "qT", [128, 16, nt], BF16)
            S.op("dve", lambda e: e.memset(qT[:], 0.0), writes=["qT"])
            kT = sb(st, "kT", [128, 4, nt], BF16)
            kvf = sb(st, "kvf", [128, nb, 512], F32)
            vb = sb(st, "vb", [128, nb, 256], BF16)
            attT = sb(st, "attT", [128, 8, nt], BF16)
            if ASTOP_G < 99:
                S.op("dve", lambda e: e.memset(attT[:], 0.0), writes=["attT"])
            for t in range(2):
                slot = wload(("win", l, C_Q // 512 + t))
                for cc in range(4):
                    c = 4 * t + cc
                    b = proj_fm(slot, cc * 128, nt)
                    ts(qT[0:64, 2 * c, :nt], ps[0:64, b, :nt], 0.125, None, ALU.mult, None, pst(b), ["qT"])
                    ts(qT[64:128, 2 * c + 1, :nt], ps[64:128, b, :nt], 0.125, None, ALU.mult, None, pst(b), ["qT"])
            AS2 = int(os.environ.get("AS2", "99"))
            if AS2 >= 1:
                slot = wload(("win", l, C_KD // 512))
                for g in range(4):
                    b = proj_fm(slot, g * 128, nt)
                    cp("act", kT[:, g, :nt], ps[:, b, :nt], pst(b), ["kT"])
            if AS2 >= 2:
                ckv = C_Z if os.environ.get("AS3") else C_KV
                slot = wload(("win", l, ckv // 512))
            for bi in range(nb if AS2 >= 2 else 0):
                b = PSget()
                for k in range(8):
                    mm(ps[:bt, b, :], xT[:, k, bi * bt:(bi + 1) * bt], ring8(slot)[:, k, :], k == 0, k == 7,
                       ["xT", ("ring", slot)], pst(b))
                cp("act", kvf[:bt, bi, :], ps[:bt, b, :], pst(b), [("kvf", bi)])
                cp("dve", vb[:bt, bi, :], kvf[:bt, bi, 256:512], [("kvf", bi)], ["vb"])
            ASTOP = int(os.environ.get("ASTOP", "99"))
            for bi, blk in enumerate(G["blocks"]):
                if ASTOP < 2:
                    break
                if sample:
                    S.dma("sp", "o_k", ks[l, blk["seq"], :, :], kvf[:bt, bi, 0:256], reads=[("kvf", bi)])
                    S.dma("sp", "o_v", vs[l, blk["seq"], :, :], kvf[:bt, bi, 256:512], reads=[("kvf", bi)])
                elif blk["last"]:
                    S.dma("sp", "o_k", kp[l, blk["seq"], :, :], kvf[:bt, bi, 0:256], reads=[("kvf", bi)])
                    S.dma("sp", "o_v", vp[l, blk["seq"], :, :], kvf[:bt, bi, 256:512], reads=[("kvf", bi)])
            with ExitStack() as st2:
                s_sb = sb(st2, "s_sb", [128, 4, 256], F32)
                e_bf = sb(st2, "e_bf", [128, 4, 256], BF16)
                eT = sb(st2, "eT", [128, 4, 2, 128], BF16)
                att = sb(st2, "att", [128, 16, 64], BF16)
                sm = sb(st2, "asm", [128, 8, 4], F32)
                ckst = sb(st2, "ckst", [128, 4, 128], F32)
                for bi, blk in enumerate(G["blocks"]):
                    cols = slice(bi * bt, (bi + 1) * bt)
                    if ASTOP < 3:
                        break
                    if sample:
                        S.dma("sp", "ckst", ckst[:], ckd[l, blk["seq"], :, :, :], writes=["ckst"])
                        b = PSget()
                        for g in range(4):
                            tr(ps[:, b, g * 128:(g + 1) * 128], ckst[:, g, :], id32[:, :], ["ckst", "id32"], pst(b))
                        cp("act", kprev[:, l, :, :], ps[:, b, :].rearrange("p (g j) -> p g j", g=4), pst(b),
                           [("kprev", l)])
                        S.dma("pool", "vprev", vprev[:, l, :], cv[l, blk["seq"], :, :], writes=[("vprev", l)])
                    has_prev = sample or not blk["first"]
                    off = 128 if has_prev else 0
                    nk = off + bt
                    bm0 = 0 if has_prev else 128
                    for g in range(4):
                        if ASTOP < 4:
                            break
                        for hb in range(2):
                            bk = PSget()
                            pk = ps[:, bk, :].rearrange("p (h j) -> p h j", j=256)
                            for h2 in range(2):
                                hh = 2 * hb + h2
                                h = 4 * g + hh
                                if has_prev:
                                    mm(pk[:bt, h2, 0:128], qT[:, h, cols], kprev[:, l, g, :], True,
                                       True, ["qT", ("kprev", l)], pst(bk))
                                mm(pk[:bt, h2, off:off + bt], qT[:, h, cols], kT[:, g, cols],
                                   True, True, ["qT", "kT"], pst(bk))
                            tt(s_sb[:bt, 2 * hb:2 * hb + 2, :nk], pk[:bt, :, :nk],
                               BM[:bt, 4 * g + 2 * hb:4 * g + 2 * hb + 2, bm0:bm0 + nk], ALU.add,
                               pst(bk) + ["BM"], ["s_sb"])
                        if ASTOP < 5:
                            continue
                        S.op("dve", lambda e: e.tensor_reduce(out=sm[:bt, 0, :], in_=s_sb[:bt, :, :nk], axis=AX.X,
                                                              op=ALU.max), reads=["s_sb"], writes=["asm"])
                        sk = bc[:bt, l, B_SINK + 4 * g:B_SINK + 4 * g + 4]
                        tt(sm[:bt, 1, :], sm[:bt, 0, :], sk, ALU.max, ["asm", "bc"], ["asm"])
                        ts(sm[:bt, 2, :], sm[:bt, 1, :], -1.0, None, ALU.mult, None, ["asm"], ["asm"])
                        for hh in range(4):
                            act(e_bf[:bt, hh, :nk], s_sb[:bt, hh, :nk], AF.Exp, ["s_sb", "asm"], ["e_bf", "asm"],
                                bias=sm[:bt, 2, hh:hh + 1], scale=1.0, accum_out=sm[:bt, 3, hh:hh + 1])
                        tt(sm[:bt, 4, :], sk, sm[:bt, 2, :], ALU.add, ["asm", "bc"], ["asm"])
                        act(sm[:bt, 5, :], sm[:bt, 4, :], AF.Exp, ["asm"], ["asm"])
                        tt(sm[:bt, 6, :], sm[:bt, 3, :], sm[:bt, 5, :], ALU.add, ["asm"], ["asm"])
                        S.op("dve", lambda e: e.reciprocal(out=sm[:bt, 7, :], in_=sm[:bt, 6, :]), reads=["asm"],
                             writes=["asm"])
                        if ASTOP < 6:
                            continue
                        b = PSget()
                        pv = psb(b).rearrange("p (h a j) -> p h a j", h=4, a=2)
                        for hh in range(4):
                            if has_prev:
                                tr(pv[:, hh, 0, :bt], e_bf[:bt, hh, 0:128], idb[:bt, :bt], ["e_bf", "idb"], pst(b))
                            tr(pv[:bt, hh, 1, :bt], e_bf[:bt, hh, off:off + bt], idb[:bt, :bt], ["e_bf", "idb"],
                               pst(b))
                        if has_prev:
                            cp("act", eT[:, :, 0, :bt], pv[:, :, 0, :bt], pst(b), ["eT"])
                        cp("act", eT[:bt, :, 1, :bt], pv[:bt, :, 1, :bt], pst(b), ["eT"])
                        bo = PSget()
                        for hh in range(4):
                            if has_prev:
                                mm(ps[:bt, bo, hh * 64:(hh + 1) * 64], eT[:, hh, 0, :bt],
                                   vprev[:, l, g * 64:(g + 1) * 64], True, False, ["eT", ("vprev", l)], pst(bo))
                            mm(ps[:bt, bo, hh * 64:(hh + 1) * 64], eT[:bt, hh, 1, :bt],
                               vb[:bt, bi, g * 64:(g + 1) * 64], not has_prev, True, ["eT", "vb"], pst(bo))
                        tt(att[:bt, 4 * g:4 * g + 4, :], ps[:bt, bo, 0:256].rearrange("p (h d) -> p h d", h=4),
                           bcast(sm[:bt, 7, :], 2, [bt, 4, 64]), ALU.mult, pst(bo) + ["asm"], ["att"])
                    if ASTOP < 7:
                        continue
                    b = PSget()
                    a2 = att[:bt, :, :].rearrange("p h d -> p (h d)")
                    for c in range(8):
                        tr(psb(b)[:, c * 128:c * 128 + bt], a2[:, c * 128:(c + 1) * 128], idb[:bt, :bt],
                           ["att", "idb"], pst(b))
                    cp("act", attT[:, :, cols], psb(b).rearrange("p (c t) -> p c t", c=8)[:, :, :bt], pst(b),
                       ["attT"])
                    if not sample:
                        cp("dve", kprev[:, l, :, :], kT[:, :, cols], ["kT"], [("kprev", l)])
                        cp("act", vprev[:, l, :], vb[:, bi, :], ["vb"], [("vprev", l)])
                S.barrier()
            with ExitStack() as st2:
                if AS2 >= 3:
                    gate_and_out(G, l, st2, C_GATT, lambda t: ("wao", l, t), 2, 8, attT, "attT",
                                 False)
                S.barrier()

    def phase_sc(G, l):
        nt, bt, nb, nseg, segt = G["nt"], G["bt"], G["nb"], G["nseg"], G["segt"]
        sample = G["sample"]
        with ExitStack() as st:
            scv = sb(st, "scv", [128, 8, nt], BF16)
            with ExitStack() as st2:
                raw = [sb(st2, "sraw%d" % i, [128, nseg, 2 + segt], F32) for i in range(2)]
                cacc = [sb(st2, "scacc%d" % i, [128, nseg, segt], F32) for i in range(2)]
                tmpc = [sb(st2, "stmp%d" % i, [128, nt], F32) for i in range(2)]
                for half in range(2):
                    s_c = wload(("win", l, C_SCC // 512 + half))
                    s_h = wload(("win", l, C_SCH // 512 + half))
                    s_b = wload(("win", l, C_SCB // 512 + half))
                    for cc in range(4):
                        c = 4 * half + cc
                        r, a, tm = raw[c % 2], cacc[c % 2], tmpc[c % 2]
                        rt, at, tmt = "sraw%d" % (c % 2), "scacc%d" % (c % 2), "stmp%d" % (c % 2)
                        b1 = proj_fm(s_c, cc * 128, nt)
                        b2 = proj_fm(s_h, cc * 128, nt)
                        b3 = proj_fm(s_b, cc * 128, nt)
                        cp("act", tm[:, :nt], ps[:, b1, :nt], pst(b1), [tmt])
                        tt(r[:, :, 2:2 + segt], tm[:, :nt].rearrange("p (s t) -> p s t", s=nseg),
                           ps[:, b2, :nt].rearrange("p (s t) -> p s t", s=nseg), ALU.mult, [tmt] + pst(b2), [rt])
                        cp("dve", r[:, :, 0:2], hists[:, l, c, :nseg, :], [("hists", l)], [rt])
                        cp("dve", hists[:, l, c, :nseg, :], r[:, :, segt:segt + 2], [rt], [("hists", l)])
                        ts(a[:], r[:, :, 0:segt], pp[:, l, P_SCW + 3 * c:P_SCW + 3 * c + 1], None, ALU.mult, None,
                           [rt, "pp"], [at])
                        for j in range(1, 3):
                            stt(a[:], r[:, :, j:j + segt], pp[:, l, P_SCW + 3 * c + j:P_SCW + 3 * c + j + 1], a[:],
                                ALU.mult, ALU.add, [rt, at, "pp"], [at])
                        tt(scv[:, c, :nt], a[:].rearrange("p s t -> p (s t)"), ps[:, b3, :nt], ALU.mult,
                           [at] + pst(b3), ["scv"])
                for si in range(nseg):
                    blk = G["blocks"][si if sample else nb - 1]
                    if blk["last"]:
                        dst = (scs if sample else scp)[l, blk["seq"], :, :, :]
                        S.dma("sp", "o_sc", dst, hists[:, l, :, si, :], reads=[("hists", l)])
                S.barrier()
            with ExitStack() as st2:
                gate_and_out(G, l, st2, C_GSC, lambda t: ("wsc", l, t), 2, 8, scv, "scv", False)
                S.barrier()

    def layer_norm(G, l, gcol, bcol):
        nt = G["nt"]
        with ExitStack() as st:
            sq = sb(st, "lnsq", [128, 8, nt], F32)
            mean = sb(st, "lnmean", [128, nt], F32)
            var = sb(st, "lnvar", [128, nt], F32)
            rstd = sb(st, "lnrstd", [128, nt], F32)
            tmp = [sb(st, "lntmp%d" % i, [128, nt], F32) for i in range(2)]
            for k in range(8):
                act(sq[:, k, :], resT[:, k, :nt], AF.Square, [("resT", k)], ["lnsq"])
            b1 = PSget()
            for k in range(8):
                mm(ps[:, b1, :nt], ones32[:, :], resT[:, k, :nt], k == 0, k == 7, ["ones32", ("resT", k)], pst(b1))
            b2 = PSget()
            for k in range(8):
                mm(ps[:, b2, :nt], ones32[:, :], sq[:, k, :], k == 0, k == 7, ["ones32", "lnsq"], pst(b2))
            act(mean[:], ps[:, b1, :nt], AF.Copy, pst(b1), ["lnmean"], scale=1.0 / D)
            tt(var[:], mean[:], mean[:], ALU.mult, ["lnmean"], ["lnvar"])
            stt(var[:], ps[:, b2, :nt], 1.0 / D, var[:], ALU.mult, ALU.subtract, pst(b2) + ["lnvar"], ["lnvar"])
            ts(var[:], var[:], LN_EPS, None, ALU.add, None, ["lnvar"], ["lnvar"])
            act(rstd[:], var[:], AF.Sqrt, ["lnvar"], ["lnrstd"])
            S.op("dve", lambda e: e.reciprocal(out=var[:], in_=rstd[:]), reads=["lnrstd"], writes=["lnvar"])
            for k in range(8):
                t_ = tmp[k % 2]
                tk = "lntmp%d" % (k % 2)
                tt(t_[:], resT[:, k, :nt], mean[:], ALU.subtract, [("resT", k), "lnmean"], [tk])
                tt(t_[:], t_[:], var[:], ALU.mult, [tk, "lnvar"], [tk])
                ts(resT[:, k, :nt], t_[:], pp[:, l, gcol + k:gcol + k + 1], pp[:, l, bcol + k:bcol + k + 1],
                   ALU.mult, ALU.add, [tk, "pp"], [("resT", k)])
                cp("act", xT[:, k, :nt], resT[:, k, :nt], [("resT", k)], ["xT"])
            S.barrier()

    def phase_moe(G, l):
        nt, bt, nb = G["nt"], G["bt"], G["nb"]
        with ExitStack() as st:
            GT = sb(st, "GT", [32, nt], F32)
            GTm = sb(st, "GTm", [32, nt], F32)
            Gbs = sb(st, "Gbs", [128, nt], F32)
            actT = [sb(st, "actT%d" % i, [128, 8, nt], BF16) for i in range(2)]
            lg = sb(st, "lg", [128, 32], F32)
            ex = sb(st, "ex", [128, 32], F32)
            sel = sb(st, "sel", [128, 32], F32)
            Gm = sb(st, "Gm", [128, 32], F32)
            mx8 = sb(st, "mx8", [128, 8], F32)
            rsm = sb(st, "rsm", [128, 4], F32)
            gbuf = [sb(st, "mg%d" % i, [128, nt], F32) for i in range(2)]
            sgbuf = [sb(st, "msg%d" % i, [128, nt], F32) for i in range(2)]
            rbuf = [sb(st, "mr%d" % i, [128, nt], F32) for i in range(2)]
            for bi in range(nb):
                cols = slice(bi * bt, (bi + 1) * bt)
                b = PSget()
                for k in range(8):
                    mm(ps[:bt, b, 0:32], resT[:, k, cols], wr[:, l, k, :], k == 0, k == 7, [("resT", k), "wr"], pst(b))
                tt(lg[:bt, :], ps[:bt, b, 0:32], bc[:bt, l, B_RB:B_RB + 32], ALU.add, pst(b) + ["bc"], ["lg"])
                S.op("dve", lambda e: e.max(out=mx8[:bt, :], in_=lg[:bt, :]), reads=["lg"], writes=["mx8"])
                ts(sel[:bt, :], lg[:bt, :], mx8[:bt, 3:4], None, ALU.is_ge, None, ["lg", "mx8"], ["sel"])
                ts(rsm[:bt, 0:1], mx8[:bt, 0:1], -1.0, None, ALU.mult, None, ["mx8"], ["rsm"])
                act(ex[:bt, :], lg[:bt, :], AF.Exp, ["lg", "rsm"], ["ex"], bias=rsm[:bt, 0:1], scale=1.0)
                tt(ex[:bt, :], ex[:bt, :], sel[:bt, :], ALU.mult, ["ex", "sel"], ["ex"])
                S.op("dve", lambda e: e.tensor_reduce(out=rsm[:bt, 1:2], in_=ex[:bt, :], axis=AX.X, op=ALU.add),
                     reads=["ex"], writes=["rsm"])
                S.op("dve", lambda e: e.reciprocal(out=rsm[:bt, 2:3], in_=rsm[:bt, 1:2]), reads=["rsm"],
                     writes=["rsm"])
                ts(Gm[:bt, :], ex[:bt, :], rsm[:bt, 2:3], None, ALU.mult, None, ["ex", "rsm"], ["Gm"])
                b = PSget()
                tr(ps[:32, b, 0:bt], Gm[:bt, :], id32[:bt, :bt], ["Gm", "id32"], pst(b))
                cp("act", GT[:, cols], ps[:32, b, 0:bt], pst(b), ["GT"])
            for c in range(8):
                b = PSget()
                mm(ps[:, b, :nt], bdn[:, l, c * 128:(c + 1) * 128], GT[:, :nt], True, True, ["bdn", "GT"], pst(b))
                stt(resT[:, c, :nt], resT[:, c, :nt], ALPHA, ps[:, b, :nt], ALU.mult, ALU.add,
                    [("resT", c)] + pst(b), [("resT", c)])
            def gate_bcast(e_):
                ts(GTm[:, :nt], GT[:, :nt], id32[:32, e_:e_ + 1], None, ALU.mult, None, ["GT", "id32"], ["GTm"])
                b = PSget()
                mm(ps[:, b, :nt], ones32[:32, :], GTm[:, :nt], True, True, ["ones32", "GTm"], pst(b))
                cp("act", Gbs[:, :nt], ps[:, b, :nt], pst(b), ["Gbs"])

            def up_part(e_, js):
                aT = actT[e_ % 2]
                for j in js:
                    slot = wload(("wup", l, e_, j))
                    for i in range(2):
                        fi = 2 * j + i
                        gb, sg, rb = gbuf[fi % 2], sgbuf[fi % 2], rbuf[fi % 2]
                        gt_, sgt, rbt = "mg%d" % (fi % 2), "msg%d" % (fi % 2), "mr%d" % (fi % 2)
                        bg = proj_fm(slot, i * 128, nt)
                        bu = proj_fm(slot, 256 + i * 128, nt)
                        pcol = P_BUP + 16 * e_ + fi
                        ts(gb[:, :nt], ps[:, bg, :nt], pp[:, l, pcol:pcol + 1], 7.0, ALU.add, ALU.min,
                           pst(bg) + ["pp"], [gt_])
                        act(sg[:, :nt], gb[:, :nt], AF.Sigmoid, [gt_], [sgt], scale=1.702)
                        tt(gb[:, :nt], gb[:, :nt], sg[:, :nt], ALU.mult, [gt_, sgt], [gt_])
                        act(rb[:, :nt], ps[:, bu, :nt], AF.Relu, pst(bu) + ["bu7"], [rbt],
                            bias=bu7[:, l, e_, fi:fi + 1], scale=1.0)
                        ts(rb[:, :nt], rb[:, :nt], 14.0, -6.0, ALU.min, ALU.add, [rbt], [rbt])
                        tt(rb[:, :nt], rb[:, :nt], gb[:, :nt], ALU.mult, [rbt, gt_], [rbt])
                        tt(aT[:, fi, :nt], rb[:, :nt], Gbs[:, :nt], ALU.mult, [rbt, "Gbs"], [("actT", e_ % 2, fi)])

            def down_part(e_):
                aT = actT[e_ % 2]
                for j in range(2):
                    slot = wload(("wdn", l, e_, j))
                    for cc in range(4):
                        c = 4 * j + cc
                        b = PSget()
                        for fi in range(8):
                            mm(ps[:, b, :nt], ring8(slot)[:, fi, cc * 128:(cc + 1) * 128], aT[:, fi, :nt], fi == 0,
                               fi == 7, [("ring", slot), ("actT", e_ % 2, fi)], pst(b))
                        tt(resT[:, c, :nt], ps[:, b, :nt], resT[:, c, :nt], ALU.add, pst(b) + [("resT", c)],
                           [("resT", c)])

            for e_ in range(32):
                gate_bcast(e_)
                up_part(e_, (0, 1))
                if e_ > 0:
                    down_part(e_ - 1)
                up_part(e_, (2, 3))
            down_part(31)
            S.barrier()

    def store_y(G):
        nt, bt, nb = G["nt"], G["bt"], G["nb"]
        with ExitStack() as st:
            yst = [sb(st, "yst%d" % i, [128, 1024], F32) for i in range(2)]
            for bi, blk in enumerate(G["blocks"]):
                cols = slice(bi * bt, (bi + 1) * bt)
                yb = yst[bi % 2]
                yt_ = "yst%d" % (bi % 2)
                for half in range(2):
                    b = PSget()
                    for kk in range(4):
                        k = half * 4 + kk
                        tr(ps[:bt, b, kk * 128:(kk + 1) * 128], resT[:, k, cols], id32[:, :], [("resT", k), "id32"], pst(b))
                    cp("act", yb[:bt, half * 512:(half + 1) * 512], ps[:bt, b, :], pst(b),
                       [yt_])
                dst = ys[blk["seq"], :, :] if G["sample"] else yp[blk["seq"], blk["row0"]:blk["row0"] + bt, :]
                S.dma("sp", yt_, dst, yb[:bt, :], reads=[yt_])
            S.barrier()

    def emit_group(G):
        if stop < 1:
            return
        load_x(G)
        for l in range(L):
            if G["sample"]:
                for si, blk in enumerate(G["blocks"]):
                    S.dma("sp", "h_x", histx[:, l, :, si, :], sconv[l, blk["seq"], :, :, :], writes=[("histx", l)])
                    S.dma("sp", "h_s", hists[:, l, :, si, :], ssc[l, blk["seq"], :, :, :], writes=[("hists", l)])
            elif G["blocks"][0]["first"]:
                S.op("dve", lambda e: e.memset(histx[:, l, :, :, :], 0.0), writes=[("histx", l)])
                S.op("dve", lambda e: e.memset(hists[:, l, :, :, :], 0.0), writes=[("hists", l)])
            if stop >= 2:
                for _rep in range(int(os.environ.get("SSDREP", "1"))):
                    phase_ssd(G, l)
            if stop >= 3:
                phase_attn(G, l)
            if stop >= 4:
                phase_sc(G, l)
            if stop >= 5:
                layer_norm(G, l, P_L1G, P_L1B)
            if stop >= 6:
                phase_moe(G, l)
            if stop >= 7:
                layer_norm(G, l, P_L2G, P_L2B)
        if stop >= 8:
            store_y(G)

    groups = []
    for seq in range(2):
        for gi in range(n_pgroups):
            groups.append(dict(sample=False, nt=512, bt=128, nb=4, nseg=1, segt=512,
                               blocks=[dict(seq=seq, first=(gi == 0 and bi == 0),
                                            last=(gi == n_pgroups - 1 and bi == 3),
                                            chunks=[(0, 64), (64, 64)], row0=gi * 512 + bi * 128)
                                       for bi in range(4)]))
    if do_sample:
        groups.append(dict(sample=True, nt=32, bt=16, nb=2, nseg=2, segt=16,
                           blocks=[dict(seq=b, first=True, last=True, chunks=[(0, 16)], row0=0) for b in range(2)]))
    for gi_, G in enumerate(groups):
        emit_group(G)
        if gi_ == 0:
            S.barrier()
            cv_done[0] = True
    S.barrier()
    top.close()
    return nc, S


def prep_shared(inp):
    f = lambda a: np.ascontiguousarray(np.asarray(a, dtype=np.float32))
    w_in = np.asarray(inp["w_in"], np.float32)
    o = {}
    z, xbc, dt, q, k, v, scb, scc, sch, gssd, gatt, gsc = np.split(
        w_in, np.cumsum([2048, 4096, 32, 1024, 256, 256, 1024, 1024, 1024, 1024, 1024])[:], axis=-1)
    kdup = np.concatenate([np.concatenate([k[:, :, g * 64:(g + 1) * 64]] * 2, -1) for g in range(4)], -1)
    wr_ = np.concatenate([z, xbc, gssd, q, kdup, gatt, scc, sch, scb, gsc, k, v], -1)
    assert wr_.shape[-1] == DINR
    pk = lambda w, kc: f(w.reshape(w.shape[:-2] + (kc, 128, w.shape[-1])).swapaxes(-3, -2))
    o["win"] = pk(wr_, 8)
    o["wdt"] = pk(dt, 8)
    o["wso"] = pk(np.asarray(inp["w_ssd_out"], np.float32), 16)
    o["wao"] = pk(np.asarray(inp["w_attn_out"], np.float32), 8)
    o["wsc"] = pk(np.asarray(inp["w_sc_out"], np.float32), 8)
    o["wo"] = pk(np.asarray(inp["w_o"], np.float32), 8)
    wu = np.asarray(inp["w_up"], np.float32)
    gch = wu[..., :1024].reshape(2, 32, 1024, 4, 2, 128)
    uch = wu[..., 1024:].reshape(2, 32, 1024, 4, 2, 128)
    wu_r = np.concatenate([gch, uch], axis=4).reshape(2, 32, 1024, 2048)
    o["wup"] = pk(wu_r, 8)
    del wu_r, gch, uch
    o["wdn"] = pk(np.asarray(inp["w_down"], np.float32), 8)
    o["wr"] = pk(np.asarray(inp["router_w"], np.float32), 8)
    ppt = np.zeros((128, 2, NPP), np.float32)
    bct = np.zeros((128, 2, NBC), np.float32)
    for l in range(2):
        cw = np.asarray(inp["ssd_conv_w"][l], np.float32)
        ppt[:, l, P_CW:P_CW + 128] = cw.reshape(4, 32, 128).transpose(2, 1, 0).reshape(128, 128)
        ppt[:, l, P_CB:P_CB + 32] = np.asarray(inp["ssd_conv_b"][l], np.float32).reshape(32, 128).T
        ppt[:, l, P_NW:P_NW + 16] = np.asarray(inp["ssd_norm_w"][l], np.float32).reshape(16, 128).T
        sw = np.asarray(inp["sc_conv_w"][l], np.float32)
        ppt[:, l, P_SCW:P_SCW + 24] = sw.reshape(3, 8, 128).transpose(2, 1, 0).reshape(128, 24)
        for nm, col in (("ln1_g", P_L1G), ("ln1_b", P_L1B), ("ln2_g", P_L2G), ("ln2_b", P_L2B)):
            ppt[:, l, col:col + 8] = np.asarray(inp[nm][l], np.float32).reshape(8, 128).T
        bu = np.asarray(inp["b_up"][l], np.float32)
        ppt[:, l, P_BUP:P_BUP + 512] = bu.reshape(32, 16, 128).transpose(2, 0, 1).reshape(128, 512)
        bct[:, l, B_DTB:B_DTB + 32] = np.asarray(inp["ssd_dt_bias"][l], np.float32)[None, :]
        bct[:, l, B_ALOG:B_ALOG + 32] = np.asarray(inp["ssd_a_log"][l], np.float32)[None, :]
        bct[:, l, B_DD:B_DD + 32] = np.asarray(inp["ssd_d"][l], np.float32)[None, :]
        bct[:, l, B_SINK:B_SINK + 16] = np.asarray(inp["attn_sinks"][l], np.float32)[None, :]
        bct[:, l, B_RB:B_RB + 32] = np.asarray(inp["router_b"][l], np.float32)[None, :]
    o["pp"] = ppt
    o["bc"] = bct
    o["bdn"] = f(np.asarray(inp["b_down"], np.float32).transpose(1, 0, 2))
    o["tab"] = f(inp["rel_bias"])
    o.update(make_consts())
    return o


def prep_core(inp, i):
    f = lambda a: np.ascontiguousarray(np.asarray(a, dtype=np.float32))
    sl = slice(2 * i, 2 * i + 2)
    o = {}
    o["xp"] = f(inp["x_prompt"][sl])
    o["xs"] = f(inp["x_sample"][sl])
    ck = np.asarray(inp["cache_attn_k"], np.float32)[:, sl]
    o["ckd"] = f(np.concatenate([ck, ck], -1))
    o["cv"] = f(np.asarray(inp["cache_attn_v"], np.float32)[:, sl].reshape(2, 2, 128, 256))
    o["sst"] = f(np.asarray(inp["state_ssd"], np.float32)[:, sl].reshape(2, 2, 2048, 128))
    sc = np.asarray(inp["state_ssd_conv"], np.float32)[:, sl]
    o["sconv"] = f(sc.reshape(2, 2, 3, 32, 128).transpose(0, 1, 4, 3, 2))
    ss = np.asarray(inp["state_short_conv"], np.float32)[:, sl]
    o["ssc"] = f(ss.reshape(2, 2, 2, 8, 128).transpose(0, 1, 4, 3, 2))
    return o


def assemble(results):
    cat = lambda key, ax: np.concatenate([r[key] for r in results], axis=ax)
    y_p = cat("yp", 0)
    y_s = cat("ys", 0)
    k_p = cat("kp", 1).reshape(2, 16, 128, 4, 64)
    v_p = cat("vp", 1).reshape(2, 16, 128, 4, 64)
    st_p = cat("stp", 1).reshape(2, 16, 32, 64, 128)
    cv_p = cat("cvp", 1).transpose(0, 1, 4, 3, 2).reshape(2, 16, 3, 4096)
    sc_p = cat("scp", 1).transpose(0, 1, 4, 3, 2).reshape(2, 16, 2, 1024)
    k_s = cat("ks", 1).reshape(2, 16, 16, 4, 64)
    v_s = cat("vs", 1).reshape(2, 16, 16, 4, 64)
    st_s = cat("sts", 1).reshape(2, 16, 32, 64, 128)
    cv_s = cat("cvs", 1).transpose(0, 1, 4, 3, 2).reshape(2, 16, 3, 4096)
    sc_s = cat("scs", 1).transpose(0, 1, 4, 3, 2).reshape(2, 16, 2, 1024)
    return tuple(np.ascontiguousarray(a, dtype=np.float32) for a in
                 (y_p, y_s, k_p, v_p, st_p, cv_p, sc_p, k_s, v_s, st_s, cv_s, sc_s))


def kernel(**inputs):
    shared = prep_shared(inputs)
    in_maps = []
    for i in range(N_CORES):
        m = dict(shared)
        m.update(prep_core(inputs, i))
        in_maps.append(m)
    nc, _ = build_program()
    res = run_bass_kernel_spmd(nc, in_maps, core_ids=list(range(N_CORES)))
    return assemble(res.results)
```
